# Optimizing a Trainium2 kernel written in Bass

```python
import jax, jax.numpy as jnp
from jax import lax
import numpy as np

D_MODEL = 2048
BATCH = 4
SEQ = 8192
DEPTH = 4

N_MIXERS = 3
N_A = (DEPTH + 2) // N_MIXERS
N_B = (DEPTH + 1) // N_MIXERS
N_C = DEPTH // N_MIXERS
SHORT_CONV_WIDTH = 3
POOL_WINDOWS = (2, 4, 8, 16)
N_POOL_GROUPS = len(POOL_WINDOWS)
POOL_GROUP_DIM = D_MODEL // N_POOL_GROUPS
CONFORMER_CONV_WIDTH = 31
FFN_CONV_WIDTH = 3
FF_MULTIPLE = 256
D_FF = FF_MULTIPLE * (-(-(2 * 4 * D_MODEL // 3) // FF_MULTIPLE))
RMS_EPS = 1e-5
LN_EPS = 1e-5

kernel_name = "hybrid_shortconv_pool_conformer_trunk"


def rms_norm(x, g):
    xf = x.astype(jnp.float32)
    y = xf * lax.rsqrt(jnp.mean(xf * xf, axis=-1, keepdims=True) + RMS_EPS)
    return (y * g.astype(jnp.float32)).astype(x.dtype)


def layer_norm(x, g, b):
    xf = x.astype(jnp.float32)
    mu = jnp.mean(xf, axis=-1, keepdims=True)
    xc = xf - mu
    var = jnp.mean(xc * xc, axis=-1, keepdims=True)
    y = xc * lax.rsqrt(var + LN_EPS) * g.astype(jnp.float32) + b.astype(jnp.float32)
    return y.astype(x.dtype)


def causal_depthwise_conv(x, w):
    k, c = w.shape
    return lax.conv_general_dilated(
        x, w[:, None, :].astype(x.dtype), window_strides=(1,), padding=((k - 1, 0),),
        dimension_numbers=("NWC", "WIO", "NWC"), feature_group_count=c)


def short_conv_mixer(h, w_in, conv_w, w_out):
    b_gate, c_gate, v = jnp.split(h @ w_in, 3, axis=-1)
    u = causal_depthwise_conv(c_gate * v, conv_w)
    return (b_gate * u) @ w_out


def pooling_mixer(h, w_group, scale):
    bsz, s, d = h.shape
    hf = h.astype(jnp.float32)
    csum = jnp.cumsum(hf, axis=1)
    count = jnp.arange(1, s + 1, dtype=jnp.float32)[None, :, None]
    outs = []
    for g, win in enumerate(POOL_WINDOWS):
        sl = slice(g * POOL_GROUP_DIM, (g + 1) * POOL_GROUP_DIM)
        cg = csum[..., sl]
        lag = jnp.pad(cg, ((0, 0), (win, 0), (0, 0)))[:, :s]
        mean = (cg - lag) / jnp.minimum(count, float(win))
        outs.append(mean - hf[..., sl])
    pooled = jnp.stack(outs, axis=2).astype(h.dtype)
    mixed = jnp.einsum("bsgi,gio->bsgo", pooled, w_group).reshape(bsz, s, d)
    return mixed * scale


def conformer_conv_mixer(h, w_pw1, b_pw1, conv_w, conv_b, ln_g, ln_b, w_pw2, b_pw2):
    a, gate = jnp.split(h @ w_pw1 + b_pw1, 2, axis=-1)
    u = a * jax.nn.sigmoid(gate)
    u = causal_depthwise_conv(u, conv_w) + conv_b
    u = layer_norm(u, ln_g, ln_b)
    return jax.nn.silu(u) @ w_pw2 + b_pw2


def conv_glu_ffn(h, w_gate, w_up, conv_w, conv_b, w_down):
    a = causal_depthwise_conv(h @ w_gate, conv_w) + conv_b
    return (jax.nn.silu(a) * (h @ w_up)) @ w_down


def setup_inputs(seed: int = 0) -> dict:
    key = jax.random.key(seed)
    ks = jax.random.split(key, 32)
    D, F = D_MODEL, D_FF
    nrm = lambda k, shape, fan: jax.random.normal(k, shape, jnp.float32) * (fan ** -0.5)
    gain = lambda k, shape: 1.0 + 0.02 * jax.random.normal(k, shape, jnp.float32)
    small = lambda k, shape: 0.01 * jax.random.normal(k, shape, jnp.float32)
    return {
        "x": jax.random.normal(ks[0], (BATCH, SEQ, D), jnp.float32),
        "mix_norm_g": gain(ks[1], (DEPTH, D)),
        "ffn_norm_g": gain(ks[2], (DEPTH, D)),
        "a_w_in": nrm(ks[3], (N_A, D, 3 * D), D),
        "a_conv_w": nrm(ks[4], (N_A, SHORT_CONV_WIDTH, D), SHORT_CONV_WIDTH),
        "a_w_out": nrm(ks[5], (N_A, D, D), D),
        "b_w_group": nrm(ks[6], (N_B, N_POOL_GROUPS, POOL_GROUP_DIM, POOL_GROUP_DIM), POOL_GROUP_DIM),
        "b_scale": 1.0 + 0.1 * jax.random.normal(ks[7], (N_B, D), jnp.float32),
        "c_w_pw1": nrm(ks[8], (N_C, D, 2 * D), D),
        "c_b_pw1": small(ks[9], (N_C, 2 * D)),
        "c_conv_w": nrm(ks[10], (N_C, CONFORMER_CONV_WIDTH, D), CONFORMER_CONV_WIDTH),
        "c_conv_b": small(ks[11], (N_C, D)),
        "c_ln_g": gain(ks[12], (N_C, D)),
        "c_ln_b": small(ks[13], (N_C, D)),
        "c_w_pw2": nrm(ks[14], (N_C, D, D), D),
        "c_b_pw2": small(ks[15], (N_C, D)),
        "f_w_gate": nrm(ks[16], (DEPTH, D, F), D),
        "f_w_up": nrm(ks[17], (DEPTH, D, F), D),
        "f_conv_w": nrm(ks[18], (DEPTH, FFN_CONV_WIDTH, F), FFN_CONV_WIDTH),
        "f_conv_b": small(ks[19], (DEPTH, F)),
        "f_w_down": nrm(ks[20], (DEPTH, F, D), F),
        "final_norm_g": gain(ks[21], (D,)),
    }


def reference(x, mix_norm_g, ffn_norm_g, a_w_in, a_conv_w, a_w_out, b_w_group, b_scale,
              c_w_pw1, c_b_pw1, c_conv_w, c_conv_b, c_ln_g, c_ln_b, c_w_pw2, c_b_pw2,
              f_w_gate, f_w_up, f_conv_w, f_conv_b, f_w_down, final_norm_g):
    for i in range(DEPTH):
        kind, j = i % N_MIXERS, i // N_MIXERS
        h = rms_norm(x, mix_norm_g[i])
        if kind == 0:
            y = short_conv_mixer(h, a_w_in[j], a_conv_w[j], a_w_out[j])
        elif kind == 1:
            y = pooling_mixer(h, b_w_group[j], b_scale[j])
        else:
            y = conformer_conv_mixer(h, c_w_pw1[j], c_b_pw1[j], c_conv_w[j], c_conv_b[j],
                                     c_ln_g[j], c_ln_b[j], c_w_pw2[j], c_b_pw2[j])
        x = x + y
        h = rms_norm(x, ffn_norm_g[i])
        x = x + conv_glu_ffn(h, f_w_gate[i], f_w_up[i], f_conv_w[i], f_conv_b[i], f_w_down[i])
    return rms_norm(x, final_norm_g)
```

```python
import numpy as np
import concourse.bass as bass
import concourse.mybir as mybir
from concourse.bass_utils import run_bass_kernel_spmd

F32 = mybir.dt.float32
BF16 = mybir.dt.bfloat16
AF = mybir.ActivationFunctionType
ALU = mybir.AluOpType

D = 2048
F = 5632
ND = D // 128
NF = F // 128
NQ = 4
FQ = NF // NQ
DEPTH = 4
HALO = 64
TOK_PER_CORE = 4096
RMS_EPS = 1e-5
LN_EPS = 1e-5
POOL_WINDOWS = (2, 4, 8, 16)
CW = 31
MARG = 32
NSLOT = 7
SLOT_ELEMS = 2048
NNS = 5


def wall_layout():
    lay = {}
    off = 0

    def add(name, n):
        nonlocal off
        lay[name] = (off, n)
        off += n

    for i in range(DEPTH):
        kind, j = i % 3, i // 3
        if kind == 0:
            for nb in range(3 * ND):
                add(("a_in", j, nb), D)
            for m in range(ND):
                add(("a_out", j, m), D)
        elif kind == 1:
            for g in range(4):
                for m in range(4):
                    add(("b_w", j, g, m), 512)
        else:
            for nb in range(2 * ND):
                add(("c_pw1", j, nb), D)
            for m in range(ND):
                add(("c_pw2", j, m), D)
        for f in range(NF):
            add(("f_gate", i, f), D)
            add(("f_up", i, f), D)
        for q in range(NQ):
            for m in range(ND):
                add(("f_down", i, q, m), FQ * 128)
    return lay, off


def _blocks(W):
    K, N = W.shape
    return W.reshape(K // 128, 128, N // 128, 128).transpose(2, 1, 0, 3).reshape(N // 128, 128, K)


def pack_wall(inp):
    lay, total = wall_layout()
    wall = np.empty((128, total), np.float32)

    def put(name, arr):
        o, n = lay[name]
        assert arr.shape == (128, n), (name, arr.shape, n)
        wall[:, o:o + n] = arr

    for i in range(DEPTH):
        kind, j = i % 3, i // 3
        if kind == 0:
            b = _blocks(np.asarray(inp["a_w_in"][j]))
            for nb in range(3 * ND):
                put(("a_in", j, nb), b[nb])
            b = _blocks(np.asarray(inp["a_w_out"][j]))
            for m in range(ND):
                put(("a_out", j, m), b[m])
        elif kind == 1:
            for g in range(4):
                b = _blocks(np.asarray(inp["b_w_group"][j][g]))
                for m in range(4):
                    put(("b_w", j, g, m), b[m])
        else:
            b = _blocks(np.asarray(inp["c_w_pw1"][j]))
            for nb in range(2 * ND):
                put(("c_pw1", j, nb), b[nb])
            b = _blocks(np.asarray(inp["c_w_pw2"][j]))
            for m in range(ND):
                put(("c_pw2", j, m), b[m])
        b = _blocks(np.asarray(inp["f_w_gate"][i]))
        for f in range(NF):
            put(("f_gate", i, f), b[f])
        b = _blocks(np.asarray(inp["f_w_up"][i]))
        for f in range(NF):
            put(("f_up", i, f), b[f])
        wd = np.asarray(inp["f_w_down"][i])
        for q in range(NQ):
            b = _blocks(wd[q * FQ * 128:(q + 1) * FQ * 128, :])
            for m in range(ND):
                put(("f_down", i, q, m), b[m])
    return wall


def param_layout():
    lay = {}
    off = 0

    def add(name, ncols):
        nonlocal off
        lay[name] = off
        off += ncols

    for i in range(DEPTH):
        add(("mix_g", i), ND)
        add(("ffn_g", i), ND)
        for k in range(3):
            add(("f_cw", i, k), NF)
        add(("f_cb", i), NF)
    add(("fin_g",), ND)
    for j in range(2):
        for k in range(3):
            add(("a_cw", j, k), ND)
    add(("b_scale",), ND)
    add(("c_b1",), 2 * ND)
    for k in range(CW):
        add(("c_cw", k), ND)
    add(("c_cb",), ND)
    add(("c_lng",), ND)
    add(("c_lnb",), ND)
    add(("c_b2",), ND)
    return lay, off


def pack_params(inp):
    lay, total = param_layout()
    P = np.zeros((128, total), np.float32)

    def put(name, vec):
        vec = np.asarray(vec, np.float32).reshape(-1)
        n = vec.shape[0] // 128
        P[:, lay[name]:lay[name] + n] = vec.reshape(n, 128).T

    for i in range(DEPTH):
        put(("mix_g", i), inp["mix_norm_g"][i])
        put(("ffn_g", i), inp["ffn_norm_g"][i])
        for k in range(3):
            put(("f_cw", i, k), inp["f_conv_w"][i][k])
        put(("f_cb", i), inp["f_conv_b"][i])
    put(("fin_g",), inp["final_norm_g"])
    for j in range(2):
        for k in range(3):
            put(("a_cw", j, k), inp["a_conv_w"][j][k])
    put(("b_scale",), inp["b_scale"][0])
    put(("c_b1",), inp["c_b_pw1"][0])
    for k in range(CW):
        put(("c_cw", k), inp["c_conv_w"][0][k])
    put(("c_cb",), inp["c_conv_b"][0])
    put(("c_lng",), inp["c_ln_g"][0])
    put(("c_lnb",), inp["c_ln_b"][0])
    put(("c_b2",), inp["c_b_pw2"][0])
    return P


def make_aux(first_half):
    aux = np.zeros((128, 128), np.float32)
    aux[:, 0:64] = 0.0 if first_half else 1.0
    for g, w in enumerate(POOL_WINDOWS):
        for t in range(16):
            cnt = min(t + 1, w) if first_half else w
            aux[:, 64 + g * 16 + t] = np.float32(1.0) / np.float32(cnt)
    return aux


def default_sts():
    sts = [[(0, HALO), (HALO, 512), (HALO + 512, 512)]]
    for _ in range(3):
        sts.append([(0, 512), (512, 512)])
    return sts


class Buf:
    __slots__ = ("name", "w", "r")

    def __init__(self, name):
        self.name = name
        self.w = None
        self.r = {}


class Eng:
    def __init__(self, name, sem):
        self.name = name
        self.sem = sem
        self.n = 0
        self.ops = []
        self.waited = {}


class DSem:
    def __init__(self, sem):
        self.sem = sem
        self.n = 0


class Prog:
    def __init__(self, nc):
        self.nc = nc
        self.eng = {}
        for name in ("pe", "act", "dve", "pool", "sp"):
            self.eng[name] = Eng(name, nc.alloc_semaphore("sem_" + name))

    def _deps(self, eng, reads, writes):
        need = {}

        def add(tok, raw):
            if tok is None:
                return
            sem, val, src = tok
            if src == eng.name and (not raw or eng.name == "pe"):
                return
            if need.get(sem, 0) < val:
                need[sem] = val

        for b in reads:
            add(b.w, True)
        for b in writes:
            add(b.w, False)
            for sem, (val, src) in b.r.items():
                add((sem, val, src), False)
        for sem, val in need.items():
            if eng.waited.get(sem, 0) < val:
                eng.waited[sem] = val
                eng.ops.append(("wait", sem, val))

    def _commit(self, tok, reads, writes):
        sem, val, src = tok
        for b in reads:
            cur = b.r.get(sem)
            if cur is None or cur[0] < val:
                b.r[sem] = (val, src)
        for b in writes:
            b.w = tok
            b.r = {}

    def op(self, ename, fn, reads=(), writes=()):
        eng = self.eng[ename]
        self._deps(eng, reads, writes)
        eng.n += 1
        eng.ops.append(("inc", fn))
        self._commit((eng.sem, eng.n, eng.name), reads, writes)

    def mm_group(self, out_ap, out_buf, pairs, reads, start=True, stop=True):
        eng = self.eng["pe"]
        self._deps(eng, reads, [out_buf])
        n = len(pairs)
        for k, (lhsT, rhs) in enumerate(pairs):
            st = start and k == 0
            sp = stop and k == n - 1
            fn = (lambda e, o=out_ap, l=lhsT, r=rhs, st=st, sp=sp: e.matmul(o, l, r, start=st, stop=sp))
            if k == n - 1:
                eng.n += 1
                eng.ops.append(("inc", fn))
            else:
                eng.ops.append(("plain", fn))
        self._commit((eng.sem, eng.n, eng.name), reads, [out_buf])

    def dma(self, qname, dsem, out_ap, in_ap, reads=(), writes=(), **kw):
        eng = self.eng[qname]
        self._deps(eng, reads, writes)
        dsem.n += 16
        eng.ops.append(("dma", out_ap, in_ap, dsem.sem, kw))
        self._commit((dsem.sem, dsem.n, "dma"), reads, writes)

    def final_wait(self, qname, bufs):
        eng = self.eng[qname]
        self._deps(eng, bufs, [])

    def emit(self, block):
        def run(ename):
            eng = self.eng[ename]

            def body(e):
                for o in eng.ops:
                    k = o[0]
                    if k == "wait":
                        e.wait_ge(o[1], o[2])
                    elif k == "inc":
                        o[1](e).then_inc(eng.sem, 1)
                    elif k == "plain":
                        o[1](e)
                    else:
                        e.dma_start(out=o[1], in_=o[2], **o[4]).then_inc(o[3], 16)
            return body

        block.tensor(run("pe"))
        block.scalar(run("act"))
        block.vector(run("dve"))
        block.gpsimd(run("pool"))
        block.sync(run("sp"))


def build_program(sts=None, layers=(0, 1, 2, 3), final=True, skip_ffn=False):
    sts = sts or default_sts()
    st_w = [sum(w for _, w in st) for st in sts]
    SMAX = max(st_w)
    NTOK = sum(st_w)
    NREAL = NTOK - HALO
    wlay, wtotal = wall_layout()
    play, ptotal = param_layout()

    nc = bass.Bass("TRN2", target_bir_lowering=False)
    xT = nc.dram_tensor("xT", [D, NTOK], F32, kind="ExternalInput").ap()
    wall = nc.dram_tensor("wall", [128, wtotal], F32, kind="ExternalInput").ap()
    par = nc.dram_tensor("par", [128, ptotal], F32, kind="ExternalInput").ap()
    auxd = nc.dram_tensor("aux", [128, 128], F32, kind="ExternalInput").ap()
    outT = nc.dram_tensor("outT", [D, NREAL], F32, kind="ExternalOutput").ap()

    pg = Prog(nc)

    X = nc.alloc_sbuf_tensor("X", [128, ND, SMAX], F32)
    H = nc.alloc_sbuf_tensor("H", [128, ND, SMAX], BF16)
    CO = nc.alloc_sbuf_tensor("A", [128, ND * SMAX // 2], F32)
    Abf = CO.bitcast(BF16)

    def As(c, off, w):
        return Abf[:, c * SMAX + off:c * SMAX + off + w]

    def Hs(c, off, w):
        return H[:, c, off:off + w]
    WR = [nc.alloc_sbuf_tensor(f"WR{s}", [128, SLOT_ELEMS], BF16) for s in range(NSLOT)]
    STG = [nc.alloc_sbuf_tensor(f"STG{s}", [128, MARG + SMAX], F32) for s in range(3)]
    NS = [nc.alloc_sbuf_tensor(f"NS{s}", [128, 512], F32) for s in range(NNS)]
    RSTD = nc.alloc_sbuf_tensor("RSTD", [128, SMAX], F32)
    P = nc.alloc_sbuf_tensor("P", [128, ptotal], F32)
    AUX = nc.alloc_sbuf_tensor("AUX", [128, 128], F32)
    ONES = nc.alloc_sbuf_tensor("ONES", [128, 128], F32)
    GST = nc.alloc_sbuf_tensor("GST", [128, DEPTH, NF * 2], F32)
    CVST = nc.alloc_sbuf_tensor("CVST", [128, 2, ND * 2], F32)
    HST = nc.alloc_sbuf_tensor("HST", [128, ND, 16], F32)
    UST = nc.alloc_sbuf_tensor("UST", [128, ND, CW - 1], F32)
    PS = [nc.alloc_psum_tensor(f"PS{b}", [128, 512], F32) for b in range(8)]

    bX = {}
    bH = {}
    bA = {}
    bWR = [Buf(f"WR{s}") for s in range(NSLOT)]
    bSTG = [Buf(f"STG{s}") for s in range(3)]
    bNS = [Buf(f"NS{s}") for s in range(NNS)]
    bPS = [Buf(f"PS{b}") for b in range(8)]
    bRSTD = {}
    bP = Buf("P")
    bAUX = Buf("AUX")
    bONES = Buf("ONES")
    bGST = [[Buf(f"GST{i}_{f}") for f in range(NF)] for i in range(DEPTH)]
    bCVST = [[Buf(f"CVST{j}_{c}") for c in range(ND)] for j in range(2)]
    bHST = [Buf(f"HST{c}") for c in range(ND)]
    bUST = [Buf(f"UST{c}") for c in range(ND)]
    bOUT = Buf("outT")

    def gb(dct, pre, c, ti):
        k = (c, ti)
        if k not in dct:
            dct[k] = Buf(f"{pre}{c}_{ti}")
        return dct[k]

    dsW = [DSem(nc.alloc_semaphore(f"dw{s}")) for s in range(NSLOT)]
    dsX = [DSem(nc.alloc_semaphore(f"dx{c}")) for c in range(ND)]
    dsP = DSem(nc.alloc_semaphore("dpar"))
    dsAux = DSem(nc.alloc_semaphore("daux"))
    dsO = [DSem(nc.alloc_semaphore(f"do{s}")) for s in range(NNS)]

    state = {"ns": 0, "ps": 0, "wr": 0, "stg": 0}
    pinned = set()

    def ns():
        i = state["ns"]
        state["ns"] = (i + 1) % NNS
        return i

    def psb():
        while True:
            i = state["ps"]
            state["ps"] = (i + 1) % 8
            if i not in pinned:
                return i

    def stg():
        i = state["stg"]
        state["stg"] = (i + 1) % 3
        return i

    def wpiece(name):
        o, n = wlay[name]
        s = state["wr"]
        state["wr"] = (s + 1) % NSLOT
        pg.dma("pool", dsW[s], WR[s][:, 0:n], wall[:, o:o + n], reads=[], writes=[bWR[s]])
        return WR[s], bWR[s], n

    def pcol(name, c=0):
        o = play[name] + c
        return P[:, o:o + 1]

    pg.dma("sp", dsP, P[:, :], par[:, :], writes=[bP])
    pg.dma("sp", dsAux, AUX[:, :], auxd[:, :], writes=[bAUX])
    pg.op("dve", lambda e: e.memset(ONES[:, :], 1.0), writes=[bONES])
    pg.op("dve", lambda e: e.memset(GST[:, :, :], 0.0), writes=[b for l in bGST for b in l])
    pg.op("dve", lambda e: e.memset(CVST[:, :, :], 0.0), writes=[b for l in bCVST for b in l])
    pg.op("dve", lambda e: e.memset(HST[:, :, :], 0.0), writes=bHST)
    pg.op("dve", lambda e: e.memset(UST[:, :, :], 0.0), writes=bUST)

    def norm_stats(tiles, tis):
        banks = []
        for (off, w), ti in zip(tiles, tis):
            b = psb()
            banks.append(b)
            for c in range(ND):
                s = ns()
                pg.op("act", lambda e, s=s, c=c, off=off, w=w: e.activation(
                    out=NS[s][:, 0:w], in_=X[:, c, off:off + w], func=AF.Square),
                    reads=[gb(bX, "X", c, ti)], writes=[bNS[s]])
                pg.mm_group(PS[b][:, 0:w], bPS[b], [(ONES[:, :], NS[s][:, 0:w])],
                            reads=[bONES, bNS[s]], start=(c == 0), stop=(c == ND - 1))
        for (off, w), ti, b in zip(tiles, tis, banks):
            s = ns()
            pg.op("act", lambda e, s=s, b=b, w=w: e.activation(
                out=NS[s][:, 0:w], in_=PS[b][:, 0:w], func=AF.Sqrt, bias=RMS_EPS, scale=1.0 / D),
                reads=[bPS[b]], writes=[bNS[s]])
            pg.op("dve", lambda e, s=s, off=off, w=w: e.reciprocal(out=RSTD[:, off:off + w], in_=NS[s][:, 0:w]),
                  reads=[bNS[s]], writes=[gb(bRSTD, "RSTD", 0, ti)])

    def normalize_to_H(tiles, tis, gname):
        for (off, w), ti in zip(tiles, tis):
            for c in range(ND):
                pg.op("dve", lambda e, c=c, off=off, w=w: e.scalar_tensor_tensor(
                    out=H[:, c, off:off + w], in0=X[:, c, off:off + w], scalar=pcol(gname, c),
                    in1=RSTD[:, off:off + w], op0=ALU.mult, op1=ALU.mult),
                    reads=[gb(bX, "X", c, ti), bP, gb(bRSTD, "RSTD", 0, ti)], writes=[gb(bH, "H", c, ti)])

    def proj_group(wt, wb, nk, src, sbufs, off, w):
        b = psb()
        pairs = [(wt[:, k * 128:(k + 1) * 128], src(kk, off, w)) for k, kk in enumerate(nk)]
        pg.mm_group(PS[b][:, 0:w], bPS[b], pairs, reads=[wb] + sbufs)
        return b

    def conv3_chain(sg, off, w, wcols, bias_col):
        a = ns()
        base = MARG + off
        if bias_col is not None:
            pg.op("dve", lambda e: e.tensor_scalar(
                out=NS[a][:, 0:w], in0=STG[sg][:, base:base + w], scalar1=wcols[2], scalar2=bias_col,
                op0=ALU.mult, op1=ALU.add), reads=[bSTG[sg], bP], writes=[bNS[a]])
        else:
            pg.op("dve", lambda e: e.tensor_scalar(
                out=NS[a][:, 0:w], in0=STG[sg][:, base:base + w], scalar1=wcols[2], scalar2=None,
                op0=ALU.mult), reads=[bSTG[sg], bP], writes=[bNS[a]])
        for k in (1, 0):
            sh = 2 - k
            pg.op("dve", lambda e, k=k, sh=sh: e.scalar_tensor_tensor(
                out=NS[a][:, 0:w], in0=STG[sg][:, base - sh:base - sh + w], scalar=wcols[k],
                in1=NS[a][:, 0:w], op0=ALU.mult, op1=ALU.add),
                reads=[bSTG[sg], bP, bNS[a]], writes=[bNS[a]])
        return a

    def ffn(i, tiles, tis):
        S = tiles[-1][0] + tiles[-1][1]
        norm_stats(tiles, tis)
        normalize_to_H(tiles, tis, ("ffn_g", i))
        hb = lambda ti: [gb(bH, "H", c, ti) for c in range(ND)]
        for q in range(NQ):
            for fl in range(FQ):
                f = q * FQ + fl
                wg, wgb, _ = wpiece(("f_gate", i, f))
                wu, wub, _ = wpiece(("f_up", i, f))
                sg = stg()
                pg.op("act", lambda e, sg=sg, f=f: e.activation(
                    out=STG[sg][:, MARG - 2:MARG], in_=GST[:, i, 2 * f:2 * f + 2], func=AF.Copy),
                    reads=[bGST[i][f]], writes=[bSTG[sg]])
                wc = [pcol(("f_cw", i, k), f) for k in range(3)]
                cb = pcol(("f_cb", i), f)
                for (off, w), ti in zip(tiles, tis):
                    bg = proj_group(wg, wgb, range(ND), Hs, hb(ti), off, w)
                    bu = proj_group(wu, wub, range(ND), Hs, hb(ti), off, w)
                    pg.op("act", lambda e, sg=sg, bg=bg, off=off, w=w: e.activation(
                        out=STG[sg][:, MARG + off:MARG + off + w], in_=PS[bg][:, 0:w], func=AF.Copy),
                        reads=[bPS[bg]], writes=[bSTG[sg]])
                    a = conv3_chain(sg, off, w, wc, cb)
                    s = ns()
                    pg.op("act", lambda e, a=a, s=s, w=w: e.activation(
                        out=NS[s][:, 0:w], in_=NS[a][:, 0:w], func=AF.Silu),
                        reads=[bNS[a]], writes=[bNS[s]])
                    pg.op("dve", lambda e, s=s, bu=bu, fl=fl, off=off, w=w: e.tensor_tensor(
                        out=As(fl, off, w), in0=NS[s][:, 0:w], in1=PS[bu][:, 0:w], op=ALU.mult),
                        reads=[bNS[s], bPS[bu]], writes=[gb(bA, "A", fl, ti)])
                pg.op("act", lambda e, sg=sg, f=f: e.activation(
                    out=GST[:, i, 2 * f:2 * f + 2], in_=STG[sg][:, MARG + S - 2:MARG + S], func=AF.Copy),
                    reads=[bSTG[sg]], writes=[bGST[i][f]])
            for m in range(ND):
                wd, wdb, _ = wpiece(("f_down", i, q, m))
                for (off, w), ti in zip(tiles, tis):
                    by = proj_group(wd, wdb, range(FQ), As, [gb(bA, "A", fl, ti) for fl in range(FQ)], off, w)
                    pg.op("dve", lambda e, by=by, m=m, off=off, w=w: e.tensor_tensor(
                        out=X[:, m, off:off + w], in0=X[:, m, off:off + w], in1=PS[by][:, 0:w], op=ALU.add),
                        reads=[gb(bX, "X", m, ti), bPS[by]], writes=[gb(bX, "X", m, ti)])

    def mixer_a(i, tiles, tis):
        j = i // 3
        S = tiles[-1][0] + tiles[-1][1]
        norm_stats(tiles, tis)
        normalize_to_H(tiles, tis, ("mix_g", i))
        hb = lambda ti: [gb(bH, "H", c, ti) for c in range(ND)]
        for c in range(ND):
            wB, wBb, _ = wpiece(("a_in", j, c))
            wC, wCb, _ = wpiece(("a_in", j, ND + c))
            wV, wVb, _ = wpiece(("a_in", j, 2 * ND + c))
            sg = stg()
            pg.op("act", lambda e, sg=sg, c=c: e.activation(
                out=STG[sg][:, MARG - 2:MARG], in_=CVST[:, j, 2 * c:2 * c + 2], func=AF.Copy),
                reads=[bCVST[j][c]], writes=[bSTG[sg]])
            wc = [pcol(("a_cw", j, k), c) for k in range(3)]
            for (off, w), ti in zip(tiles, tis):
                bB = proj_group(wB, wBb, range(ND), Hs, hb(ti), off, w)
                bC = proj_group(wC, wCb, range(ND), Hs, hb(ti), off, w)
                bV = proj_group(wV, wVb, range(ND), Hs, hb(ti), off, w)
                s = ns()
                pg.op("act", lambda e, s=s, bC=bC, w=w: e.activation(
                    out=NS[s][:, 0:w], in_=PS[bC][:, 0:w], func=AF.Copy),
                    reads=[bPS[bC]], writes=[bNS[s]])
                pg.op("dve", lambda e, sg=sg, s=s, bV=bV, off=off, w=w: e.tensor_tensor(
                    out=STG[sg][:, MARG + off:MARG + off + w], in0=NS[s][:, 0:w], in1=PS[bV][:, 0:w],
                    op=ALU.mult), reads=[bNS[s], bPS[bV]], writes=[bSTG[sg]])
                a = conv3_chain(sg, off, w, wc, None)
                pg.op("dve", lambda e, a=a, bB=bB, c=c, off=off, w=w: e.tensor_tensor(
                    out=As(c, off, w), in0=NS[a][:, 0:w], in1=PS[bB][:, 0:w], op=ALU.mult),
                    reads=[bNS[a], bPS[bB]], writes=[gb(bA, "A", c, ti)])
            pg.op("act", lambda e, sg=sg, c=c: e.activation(
                out=CVST[:, j, 2 * c:2 * c + 2], in_=STG[sg][:, MARG + S - 2:MARG + S], func=AF.Copy),
                reads=[bSTG[sg]], writes=[bCVST[j][c]])
        for m in range(ND):
            wo, wob, _ = wpiece(("a_out", j, m))
            for (off, w), ti in zip(tiles, tis):
                by = proj_group(wo, wob, range(ND), As, [gb(bA, "A", c, ti) for c in range(ND)], off, w)
                pg.op("dve", lambda e, by=by, m=m, off=off, w=w: e.tensor_tensor(
                    out=X[:, m, off:off + w], in0=X[:, m, off:off + w], in1=PS[by][:, 0:w], op=ALU.add),
                    reads=[gb(bX, "X", m, ti), bPS[by]], writes=[gb(bX, "X", m, ti)])

    def mixer_b(i, tiles, tis, is_st0):
        S = tiles[-1][0] + tiles[-1][1]
        norm_stats(tiles, tis)
        for c in range(ND):
            g = c // 4
            win = POOL_WINDOWS[g]
            hbuf, pa, pb = stg(), stg(), stg()
            pg.op("act", lambda e, hbuf=hbuf, c=c: e.activation(
                out=STG[hbuf][:, MARG - 16:MARG], in_=HST[:, c, :], func=AF.Copy),
                reads=[bHST[c]], writes=[bSTG[hbuf]])
            for (off, w), ti in zip(tiles, tis):
                pg.op("dve", lambda e, hbuf=hbuf, c=c, off=off, w=w: e.scalar_tensor_tensor(
                    out=STG[hbuf][:, MARG + off:MARG + off + w], in0=X[:, c, off:off + w],
                    scalar=pcol(("mix_g", i), c), in1=RSTD[:, off:off + w], op0=ALU.mult, op1=ALU.mult),
                    reads=[gb(bX, "X", c, ti), bP, gb(bRSTD, "RSTD", 0, ti)], writes=[bSTG[hbuf]])
            lo = MARG - 16
            hi = MARG + S
            src, step, dst_cycle = hbuf, 1, [pa, pb]
            k = 0
            while step < win:
                dst = dst_cycle[k % 2]
                lo2 = lo + step
                pg.op("dve", lambda e, src=src, dst=dst, lo2=lo2, step=step: e.tensor_tensor(
                    out=STG[dst][:, lo2:hi], in0=STG[src][:, lo2:hi], in1=STG[src][:, lo2 - step:hi - step],
                    op=ALU.add), reads=[bSTG[src]], writes=[bSTG[dst]])
                src, lo, step, k = dst, lo2, step * 2, k + 1
            for (off, w), ti in zip(tiles, tis):
                pg.op("dve", lambda e, src=src, hbuf=hbuf, c=c, off=off, w=w, win=win: e.scalar_tensor_tensor(
                    out=As(c, off, w), in0=STG[src][:, MARG + off:MARG + off + w], scalar=1.0 / win,
                    in1=STG[hbuf][:, MARG + off:MARG + off + w], op0=ALU.mult, op1=ALU.subtract),
                    reads=[bSTG[src], bSTG[hbuf]], writes=[gb(bA, "A", c, ti)])
            if is_st0:
                t1 = tis[1]
                s = ns()
                c0 = MARG + HALO
                pg.op("dve", lambda e, s=s, src=src, g=g: e.tensor_tensor(
                    out=NS[s][:, 0:16], in0=STG[src][:, c0:c0 + 16], in1=AUX[:, 64 + g * 16:64 + g * 16 + 16],
                    op=ALU.mult), reads=[bSTG[src], bAUX], writes=[bNS[s]])
                pg.op("dve", lambda e, s=s, hbuf=hbuf, c=c: e.tensor_tensor(
                    out=As(c, HALO, 16), in0=NS[s][:, 0:16], in1=STG[hbuf][:, c0:c0 + 16],
                    op=ALU.subtract), reads=[bNS[s], bSTG[hbuf]], writes=[gb(bA, "A", c, t1)])
            pg.op("act", lambda e, hbuf=hbuf, c=c: e.activation(
                out=HST[:, c, :], in_=STG[hbuf][:, MARG + S - 16:MARG + S], func=AF.Copy),
                reads=[bSTG[hbuf]], writes=[bHST[c]])
        for g in range(4):
            for m in range(4):
                wt, wb, _ = wpiece(("b_w", 0, g, m))
                cm = 4 * g + m
                for (off, w), ti in zip(tiles, tis):
                    by = proj_group(wt, wb, range(4 * g, 4 * g + 4), As,
                                    [gb(bA, "A", 4 * g + k, ti) for k in range(4)], off, w)
                    pg.op("dve", lambda e, by=by, cm=cm, off=off, w=w: e.scalar_tensor_tensor(
                        out=X[:, cm, off:off + w], in0=PS[by][:, 0:w], scalar=pcol(("b_scale",), cm),
                        in1=X[:, cm, off:off + w], op0=ALU.mult, op1=ALU.add),
                        reads=[bPS[by], bP, gb(bX, "X", cm, ti)], writes=[gb(bX, "X", cm, ti)])

    def mixer_c(i, tiles, tis, is_st0):
        norm_stats(tiles, tis)
        normalize_to_H(tiles, tis, ("mix_g", i))
        bCO = Buf("COall")
        for tidx, ((off, w), ti) in enumerate(zip(tiles, tis)):
            halo_tile = is_st0 and tidx == 0
            hb = [gb(bH, "H", c, ti) for c in range(ND)]
            cobufs = [Buf(f"CO{c}") for c in range(ND)]
            allA = [b for b in bA.values()]
            for c in range(ND):
                wa, wab, _ = wpiece(("c_pw1", 0, c))
                wgt, wgb, _ = wpiece(("c_pw1", 0, ND + c))
                ba = proj_group(wa, wab, range(ND), Hs, hb, off, w)
                bg = proj_group(wgt, wgb, range(ND), Hs, hb, off, w)
                s = ns()
                pg.op("act", lambda e, s=s, bg=bg, c=c, w=w: e.activation(
                    out=NS[s][:, 0:w], in_=PS[bg][:, 0:w], func=AF.Sigmoid, bias=pcol(("c_b1",), ND + c)),
                    reads=[bPS[bg], bP], writes=[bNS[s]])
                ub = stg()
                pg.op("act", lambda e, ub=ub, c=c: e.activation(
                    out=STG[ub][:, MARG - (CW - 1):MARG], in_=UST[:, c, :], func=AF.Copy),
                    reads=[bUST[c]], writes=[bSTG[ub]])
                pg.op("dve", lambda e, ub=ub, s=s, ba=ba, c=c, w=w: e.scalar_tensor_tensor(
                    out=STG[ub][:, MARG:MARG + w], in0=PS[ba][:, 0:w], scalar=pcol(("c_b1",), c),
                    in1=NS[s][:, 0:w], op0=ALU.add, op1=ALU.mult),
                    reads=[bPS[ba], bP, bNS[s]], writes=[bSTG[ub]])
                if halo_tile:
                    pg.op("dve", lambda e, ub=ub, w=w: e.tensor_tensor(
                        out=STG[ub][:, MARG:MARG + w], in0=STG[ub][:, MARG:MARG + w], in1=AUX[:, 0:w],
                        op=ALU.mult), reads=[bSTG[ub], bAUX], writes=[bSTG[ub]])
                pg.op("act", lambda e, ub=ub, c=c, w=w: e.activation(
                    out=UST[:, c, :], in_=STG[ub][:, MARG + w - (CW - 1):MARG + w], func=AF.Copy),
                    reads=[bSTG[ub]], writes=[bUST[c]])
                co = CO[:, c * 512:c * 512 + w]
                a1 = ns()
                first = [True, True]
                for k in range(CW - 1, -1, -1):
                    ch = k % 2
                    dst = co if ch == 0 else NS[a1][:, 0:w]
                    dbuf = cobufs[c] if ch == 0 else bNS[a1]
                    src = STG[ub][:, MARG - (CW - 1) + k:MARG - (CW - 1) + k + w]
                    if first[ch]:
                        first[ch] = False
                        if ch == 0:
                            pg.op("dve", lambda e, dst=dst, src=src, k=k, c=c: e.tensor_scalar(
                                out=dst, in0=src, scalar1=pcol(("c_cw", k), c), scalar2=pcol(("c_cb",), c),
                                op0=ALU.mult, op1=ALU.add),
                                reads=[bSTG[ub], bP], writes=[dbuf] + (allA if c == 0 else []))
                        else:
                            pg.op("dve", lambda e, dst=dst, src=src, k=k, c=c: e.tensor_scalar(
                                out=dst, in0=src, scalar1=pcol(("c_cw", k), c), scalar2=None, op0=ALU.mult),
                                reads=[bSTG[ub], bP], writes=[dbuf])
                    else:
                        pg.op("dve", lambda e, dst=dst, src=src, k=k, c=c: e.scalar_tensor_tensor(
                            out=dst, in0=src, scalar=pcol(("c_cw", k), c), in1=dst, op0=ALU.mult, op1=ALU.add),
                            reads=[bSTG[ub], bP, dbuf], writes=[dbuf])
                pg.op("dve", lambda e, co=co, a1=a1, w=w: e.tensor_tensor(
                    out=co, in0=co, in1=NS[a1][:, 0:w], op=ALU.add),
                    reads=[cobufs[c], bNS[a1]], writes=[cobufs[c]])
            bsum, bsq = psb(), psb()
            for c in range(ND):
                co = CO[:, c * 512:c * 512 + w]
                s2 = ns()
                pg.op("act", lambda e, s2=s2, co=co, w=w: e.activation(
                    out=NS[s2][:, 0:w], in_=co, func=AF.Square), reads=[cobufs[c]], writes=[bNS[s2]])
                pg.mm_group(PS[bsum][:, 0:w], bPS[bsum], [(ONES[:, :], co)], reads=[bONES, cobufs[c]],
                            start=(c == 0), stop=(c == ND - 1))
                pg.mm_group(PS[bsq][:, 0:w], bPS[bsq], [(ONES[:, :], NS[s2][:, 0:w])], reads=[bONES, bNS[s2]],
                            start=(c == 0), stop=(c == ND - 1))
            mean, msq, var, rs = ns(), ns(), ns(), None
            pg.op("dve", lambda e, mean=mean, w=w, bsum=bsum: e.tensor_scalar(
                out=NS[mean][:, 0:w], in0=PS[bsum][:, 0:w], scalar1=1.0 / D, scalar2=None, op0=ALU.mult),
                reads=[bPS[bsum]], writes=[bNS[mean]])
            pg.op("dve", lambda e, mean=mean, msq=msq, w=w: e.tensor_tensor(
                out=NS[msq][:, 0:w], in0=NS[mean][:, 0:w], in1=NS[mean][:, 0:w], op=ALU.mult),
                reads=[bNS[mean]], writes=[bNS[msq]])
            pg.op("dve", lambda e, msq=msq, var=var, w=w, bsq=bsq: e.scalar_tensor_tensor(
                out=NS[var][:, 0:w], in0=PS[bsq][:, 0:w], scalar=1.0 / D, in1=NS[msq][:, 0:w],
                op0=ALU.mult, op1=ALU.subtract), reads=[bPS[bsq], bNS[msq]], writes=[bNS[var]])
            pg.op("act", lambda e, var=var, msq=msq, w=w: e.activation(
                out=NS[msq][:, 0:w], in_=NS[var][:, 0:w], func=AF.Sqrt, bias=LN_EPS, scale=1.0),
                reads=[bNS[var]], writes=[bNS[msq]])
            pg.op("dve", lambda e, var=var, msq=msq, w=w: e.reciprocal(out=NS[var][:, 0:w], in_=NS[msq][:, 0:w]),
                  reads=[bNS[msq]], writes=[bNS[var]])
            rstd = var
            keep = {mean, rstd}
            for c in range(ND):
                co = CO[:, c * 512:c * 512 + w]
                z = ns()
                while z in keep:
                    z = ns()
                pg.op("dve", lambda e, z=z, co=co, w=w, mean=mean: e.tensor_tensor(
                    out=NS[z][:, 0:w], in0=co, in1=NS[mean][:, 0:w], op=ALU.subtract),
                    reads=[cobufs[c], bNS[mean]], writes=[bNS[z]])
                pg.op("dve", lambda e, z=z, w=w, rstd=rstd: e.tensor_tensor(
                    out=NS[z][:, 0:w], in0=NS[z][:, 0:w], in1=NS[rstd][:, 0:w], op=ALU.mult),
                    reads=[bNS[z], bNS[rstd]], writes=[bNS[z]])
                pg.op("act", lambda e, z=z, c=c, off=off, w=w: e.activation(
                    out=H[:, c, off:off + w], in_=NS[z][:, 0:w], func=AF.Silu,
                    bias=pcol(("c_lnb",), c), scale=pcol(("c_lng",), c)),
                    reads=[bNS[z], bP], writes=[hb[c]])
            for m in range(ND):
                wt, wb, _ = wpiece(("c_pw2", 0, m))
                by = proj_group(wt, wb, range(ND), Hs, hb, off, w)
                pg.op("dve", lambda e, by=by, m=m, off=off, w=w: e.scalar_tensor_tensor(
                    out=X[:, m, off:off + w], in0=PS[by][:, 0:w], scalar=pcol(("c_b2",), m),
                    in1=X[:, m, off:off + w], op0=ALU.add, op1=ALU.add),
                    reads=[bPS[by], bP, gb(bX, "X", m, ti)], writes=[gb(bX, "X", m, ti)])
                if halo_tile:
                    pg.op("dve", lambda e, m=m, off=off, w=w: e.tensor_tensor(
                        out=X[:, m, off:off + w], in0=X[:, m, off:off + w], in1=AUX[:, 0:w], op=ALU.mult),
                        reads=[gb(bX, "X", m, ti), bAUX], writes=[gb(bX, "X", m, ti)])
            for b in bA.values():
                for cb_ in cobufs:
                    for sem, (val, src) in cb_.r.items():
                        cur = b.r.get(sem)
                        if cur is None or cur[0] < val:
                            b.r[sem] = (val, src)
                    if cb_.w is not None:
                        sem, val, src = cb_.w
                        cur = b.r.get(sem)
                        if cur is None or cur[0] < val:
                            b.r[sem] = (val, src)

    def final_norm(tiles, tis, st_tok0, is_st0):
        norm_stats(tiles, tis)
        for tidx, ((off, w), ti) in enumerate(zip(tiles, tis)):
            if is_st0 and tidx == 0:
                continue
            o0 = st_tok0 + off - HALO
            for c in range(ND):
                s = ns()
                pg.op("dve", lambda e, s=s, c=c, off=off, w=w: e.scalar_tensor_tensor(
                    out=NS[s][:, 0:w], in0=X[:, c, off:off + w], scalar=pcol(("fin_g",), c),
                    in1=RSTD[:, off:off + w], op0=ALU.mult, op1=ALU.mult),
                    reads=[gb(bX, "X", c, ti), bP, gb(bRSTD, "RSTD", 0, ti)], writes=[bNS[s]])
                pg.dma("sp", dsO[s], outT[c * 128:(c + 1) * 128, o0:o0 + w], NS[s][:, 0:w],
                       reads=[bNS[s]], writes=[])

    def raw_out(tiles, tis, st_tok0, is_st0):
        for tidx, ((off, w), ti) in enumerate(zip(tiles, tis)):
            if is_st0 and tidx == 0:
                continue
            o0 = st_tok0 + off - HALO
            for c in range(ND):
                pg.dma("sp", dsO[c % NNS], outT[c * 128:(c + 1) * 128, o0:o0 + w], X[:, c, off:off + w],
                       reads=[gb(bX, "X", c, ti)], writes=[])

    tok0 = 0
    tile_id = 0
    for si, tiles in enumerate(sts):
        tis = list(range(len(tiles)))
        is_st0 = si == 0
        S = st_w[si]
        for c in range(ND):
            pg.dma("sp", dsX[c], X[:, c, 0:S], xT[c * 128:(c + 1) * 128, tok0:tok0 + S],
                   writes=[gb(bX, "X", c, ti) for ti in tis])
        for i in layers:
            kind = i % 3
            if kind == 0:
                mixer_a(i, tiles, tis)
            elif kind == 1:
                mixer_b(i, tiles, tis, is_st0)
            else:
                mixer_c(i, tiles, tis, is_st0)
            if not skip_ffn:
                ffn(i, tiles, tis)
        if final:
            final_norm(tiles, tis, tok0, is_st0)
        else:
            raw_out(tiles, tis, tok0, is_st0)
        tok0 += S
    for d in dsO:
        if d.n:
            pg.eng["sp"].ops.append(("wait", d.sem, d.n))

    with nc.Block() as block:
        pg.emit(block)
    return nc, dict(NTOK=NTOK, NREAL=NREAL)


_CACHE = {}


def kernel(**inputs):
    inp = {k: np.asarray(v) for k, v in inputs.items()}
    x = inp["x"]
    B, S, _ = x.shape
    n_cores = 8
    halves = S // TOK_PER_CORE
    wallh = pack_wall(inp)
    parh = pack_params(inp)
    in_maps = []
    for core in range(n_cores):
        b, hf = core // halves, core % halves
        t0 = hf * TOK_PER_CORE
        xt = np.zeros((D, HALO + TOK_PER_CORE), np.float32)
        if hf > 0:
            xt[:, :HALO] = x[b, t0 - HALO:t0, :].T
        xt[:, HALO:] = x[b, t0:t0 + TOK_PER_CORE, :].T
        in_maps.append({"xT": xt, "wall": wallh, "par": parh, "aux": make_aux(hf == 0)})
    if "nc" not in _CACHE:
        _CACHE["nc"] = build_program()[0]
    res = run_bass_kernel_spmd(_CACHE["nc"], in_maps, core_ids=list(range(n_cores)))
    out = np.empty((B, S, D), np.float32)
    for core in range(n_cores):
        b, hf = core // halves, core % halves
        t0 = hf * TOK_PER_CORE
        out[b, t0:t0 + TOK_PER_CORE, :] = np.asarray(res.results[core]["outT"]).T
    return out
```

```python
import numpy as np
import concourse.bass as bass
import concourse.mybir as mybir
from concourse.bass_utils import run_bass_kernel_spmd

F32 = mybir.dt.float32
BF16 = mybir.dt.bfloat16
AF = mybir.ActivationFunctionType
ALU = mybir.AluOpType

D = 2048
F = 5632
ND = D // 128
NF = F // 128
NQ = 4
FQ = NF // NQ
DEPTH = 4
HALO = 64
TOK_PER_CORE = 4096
RMS_EPS = 1e-5
LN_EPS = 1e-5
POOL_WINDOWS = (2, 4, 8, 16)
CW = 31
MARG = 32
NSLOT = 7
SLOT_ELEMS = 2048
NNS = 5


def wall_layout():
    lay = {}
    off = 0

    def add(name, n):
        nonlocal off
        lay[name] = (off, n)
        off += n

    for i in range(DEPTH):
        kind, j = i % 3, i // 3
        if kind == 0:
            for nb in range(3 * ND):
                add(("a_in", j, nb), D)
            for m in range(ND):
                add(("a_out", j, m), D)
        elif kind == 1:
            for g in range(4):
                for m in range(4):
                    add(("b_w", j, g, m), 512)
        else:
            for nb in range(2 * ND):
                add(("c_pw1", j, nb), D)
            for m in range(ND):
                add(("c_pw2", j, m), D)
        for f in range(NF):
            add(("f_gate", i, f), D)
            add(("f_up", i, f), D)
        for q in range(NQ):
            for m in range(ND):
                add(("f_down", i, q, m), FQ * 128)
    return lay, off


def _blocks(W):
    K, N = W.shape
    return W.reshape(K // 128, 128, N // 128, 128).transpose(2, 1, 0, 3).reshape(N // 128, 128, K)


def pack_wall(inp):
    lay, total = wall_layout()
    wall = np.empty((128, total), np.float32)

    def put(name, arr):
        o, n = lay[name]
        assert arr.shape == (128, n), (name, arr.shape, n)
        wall[:, o:o + n] = arr

    for i in range(DEPTH):
        kind, j = i % 3, i // 3
        if kind == 0:
            b = _blocks(np.asarray(inp["a_w_in"][j]))
            for nb in range(3 * ND):
                put(("a_in", j, nb), b[nb])
            b = _blocks(np.asarray(inp["a_w_out"][j]))
            for m in range(ND):
                put(("a_out", j, m), b[m])
        elif kind == 1:
            for g in range(4):
                b = _blocks(np.asarray(inp["b_w_group"][j][g]))
                for m in range(4):
                    put(("b_w", j, g, m), b[m])
        else:
            b = _blocks(np.asarray(inp["c_w_pw1"][j]))
            for nb in range(2 * ND):
                put(("c_pw1", j, nb), b[nb])
            b = _blocks(np.asarray(inp["c_w_pw2"][j]))
            for m in range(ND):
                put(("c_pw2", j, m), b[m])
        b = _blocks(np.asarray(inp["f_w_gate"][i]))
        for f in range(NF):
            put(("f_gate", i, f), b[f])
        b = _blocks(np.asarray(inp["f_w_up"][i]))
        for f in range(NF):
            put(("f_up", i, f), b[f])
        wd = np.asarray(inp["f_w_down"][i])
        for q in range(NQ):
            b = _blocks(wd[q * FQ * 128:(q + 1) * FQ * 128, :])
            for m in range(ND):
                put(("f_down", i, q, m), b[m])
    return wall


def param_layout():
    lay = {}
    off = 0

    def add(name, ncols):
        nonlocal off
        lay[name] = off
        off += ncols

    for i in range(DEPTH):
        add(("mix_g", i), ND)
        add(("ffn_g", i), ND)
        for k in range(3):
            add(("f_cw", i, k), NF)
        add(("f_cb", i), NF)
    add(("fin_g",), ND)
    for j in range(2):
        for k in range(3):
            add(("a_cw", j, k), ND)
    add(("b_scale",), ND)
    add(("c_b1",), 2 * ND)
    for k in range(CW):
        add(("c_cw", k), ND)
    add(("c_cb",), ND)
    add(("c_lng",), ND)
    add(("c_lnb",), ND)
    add(("c_b2",), ND)
    return lay, off


def pack_params(inp):
    lay, total = param_layout()
    P = np.zeros((128, total), np.float32)

    def put(name, vec):
        vec = np.asarray(vec, np.float32).reshape(-1)
        n = vec.shape[0] // 128
        P[:, lay[name]:lay[name] + n] = vec.reshape(n, 128).T

    for i in range(DEPTH):
        put(("mix_g", i), inp["mix_norm_g"][i])
        put(("ffn_g", i), inp["ffn_norm_g"][i])
        for k in range(3):
            put(("f_cw", i, k), inp["f_conv_w"][i][k])
        put(("f_cb", i), inp["f_conv_b"][i])
    put(("fin_g",), inp["final_norm_g"])
    for j in range(2):
        for k in range(3):
            put(("a_cw", j, k), inp["a_conv_w"][j][k])
    put(("b_scale",), inp["b_scale"][0])
    put(("c_b1",), inp["c_b_pw1"][0])
    for k in range(CW):
        put(("c_cw", k), inp["c_conv_w"][0][k])
    put(("c_cb",), inp["c_conv_b"][0])
    put(("c_lng",), inp["c_ln_g"][0])
    put(("c_lnb",), inp["c_ln_b"][0])
    put(("c_b2",), inp["c_b_pw2"][0])
    return P


def make_aux(first_half):
    aux = np.zeros((128, 128), np.float32)
    aux[:, 0:64] = 0.0 if first_half else 1.0
    for g, w in enumerate(POOL_WINDOWS):
        for t in range(16):
            cnt = min(t + 1, w) if first_half else w
            aux[:, 64 + g * 16 + t] = np.float32(1.0) / np.float32(cnt)
    return aux


def default_sts():
    sts = [[(0, 384), (384, 352), (736, 352)]]
    for _ in range(3):
        sts.append([(0, 512), (512, 512)])
    return sts


class Buf:
    __slots__ = ("name", "w", "r")

    def __init__(self, name):
        self.name = name
        self.w = None
        self.r = {}


class Eng:
    def __init__(self, name, sem):
        self.name = name
        self.sem = sem
        self.n = 0
        self.ops = []
        self.waited = {}


class DSem:
    def __init__(self, sem):
        self.sem = sem
        self.n = 0


class Prog:
    def __init__(self, nc):
        self.nc = nc
        self.eng = {}
        for name in ("pe", "act", "dve", "pool", "sp"):
            self.eng[name] = Eng(name, nc.alloc_semaphore("sem_" + name))

    def _deps(self, eng, reads, writes, attach=False):
        need = {}

        def add(tok, raw):
            if tok is None:
                return
            sem, val, src = tok
            if src == eng.name and (not raw or eng.name == "pe"):
                return
            if need.get(sem, 0) < val:
                need[sem] = val

        for b in reads:
            add(b.w, True)
        for b in writes:
            add(b.w, False)
            for sem, (val, src) in b.r.items():
                add((sem, val, src), False)
        todo = []
        for sem, val in need.items():
            if eng.waited.get(sem, 0) < val:
                eng.waited[sem] = val
                todo.append((sem, val))
        att = None
        if attach and todo and eng.name in ("act", "dve"):
            att = todo.pop()
        for sem, val in todo:
            eng.ops.append(("wait", sem, val))
        return att

    def _commit(self, tok, reads, writes):
        sem, val, src = tok
        for b in reads:
            cur = b.r.get(sem)
            if cur is None or cur[0] < val:
                b.r[sem] = (val, src)
        for b in writes:
            b.w = tok
            b.r = {}

    def op(self, ename, fn, reads=(), writes=()):
        eng = self.eng[ename]
        att = self._deps(eng, reads, writes, attach=True)
        eng.n += 1
        eng.ops.append(("inc", fn, att))
        self._commit((eng.sem, eng.n, eng.name), reads, writes)

    def mm_group(self, out_ap, out_buf, pairs, reads, start=True, stop=True):
        eng = self.eng["pe"]
        self._deps(eng, reads, [out_buf])
        n = len(pairs)
        for k, (lhsT, rhs) in enumerate(pairs):
            st = start and k == 0
            sp = stop and k == n - 1
            fn = (lambda e, o=out_ap, l=lhsT, r=rhs, st=st, sp=sp: e.matmul(o, l, r, start=st, stop=sp))
            if k == n - 1:
                eng.n += 1
                eng.ops.append(("inc", fn, None))
            else:
                eng.ops.append(("plain", fn))
        self._commit((eng.sem, eng.n, eng.name), reads, [out_buf])

    def dma(self, qname, dsem, out_ap, in_ap, reads=(), writes=(), **kw):
        eng = self.eng[qname]
        self._deps(eng, reads, writes)
        dsem.n += 16
        eng.ops.append(("dma", out_ap, in_ap, dsem.sem, kw))
        self._commit((dsem.sem, dsem.n, "dma"), reads, writes)

    def final_wait(self, qname, bufs):
        eng = self.eng[qname]
        self._deps(eng, bufs, [])

    def emit(self, block):
        def run(ename):
            eng = self.eng[ename]

            def body(e):
                for o in eng.ops:
                    k = o[0]
                    if k == "wait":
                        e.wait_ge(o[1], o[2])
                    elif k == "inc":
                        ins = o[1](e)
                        if o[2] is not None:
                            ins._wait_ge(o[2][0], o[2][1])
                        ins.then_inc(eng.sem, 1)
                    elif k == "plain":
                        o[1](e)
                    else:
                        e.dma_start(out=o[1], in_=o[2], **o[4]).then_inc(o[3], 16)
            return body

        block.tensor(run("pe"))
        block.scalar(run("act"))
        block.vector(run("dve"))
        block.gpsimd(run("pool"))
        block.sync(run("sp"))


def build_program(sts=None, layers=(0, 1, 2, 3), final=True, skip_ffn=False):
    sts = sts or default_sts()
    st_w = [sum(w for _, w in st) for st in sts]
    SMAX = max(st_w)
    NTOK = sum(st_w)
    NREAL = NTOK - HALO
    wlay, wtotal = wall_layout()
    play, ptotal = param_layout()

    nc = bass.Bass("TRN2", target_bir_lowering=False)
    xT = nc.dram_tensor("xT", [D, NTOK], F32, kind="ExternalInput").ap()
    wall = nc.dram_tensor("wall", [128, wtotal], F32, kind="ExternalInput").ap()
    par = nc.dram_tensor("par", [128, ptotal], F32, kind="ExternalInput").ap()
    auxd = nc.dram_tensor("aux", [128, 128], F32, kind="ExternalInput").ap()
    outT = nc.dram_tensor("outT", [D, NREAL], F32, kind="ExternalOutput").ap()

    pg = Prog(nc)

    X = nc.alloc_sbuf_tensor("X", [128, ND, SMAX], F32)
    H = nc.alloc_sbuf_tensor("H", [128, ND, SMAX], BF16)
    CO = nc.alloc_sbuf_tensor("A", [128, ND * SMAX // 2], F32)
    Abf = CO.bitcast(BF16)

    def As(c, off, w):
        return Abf[:, c * SMAX + off:c * SMAX + off + w]

    def Hs(c, off, w):
        return H[:, c, off:off + w]
    WR = [nc.alloc_sbuf_tensor(f"WR{s}", [128, SLOT_ELEMS], BF16) for s in range(NSLOT)]
    STG = [nc.alloc_sbuf_tensor(f"STG{s}", [128, MARG + SMAX], F32) for s in range(3)]
    NS = [nc.alloc_sbuf_tensor(f"NS{s}", [128, 512], F32) for s in range(NNS)]
    RSTD = nc.alloc_sbuf_tensor("RSTD", [128, SMAX], F32)
    P = nc.alloc_sbuf_tensor("P", [128, ptotal], F32)
    AUX = nc.alloc_sbuf_tensor("AUX", [128, 128], F32)
    ONES = nc.alloc_sbuf_tensor("ONES", [128, 128], F32)
    GST = nc.alloc_sbuf_tensor("GST", [128, DEPTH, NF * 2], F32)
    CVST = nc.alloc_sbuf_tensor("CVST", [128, 2, ND * 2], F32)
    HST = nc.alloc_sbuf_tensor("HST", [128, ND, 16], F32)
    UST = nc.alloc_sbuf_tensor("UST", [128, ND, CW - 1], F32)
    PS = [nc.alloc_psum_tensor(f"PS{b}", [128, 512], F32) for b in range(8)]

    bX = {}
    bH = {}
    bA = {}
    bWR = [Buf(f"WR{s}") for s in range(NSLOT)]
    bSTG = [Buf(f"STG{s}") for s in range(3)]
    bNS = [Buf(f"NS{s}") for s in range(NNS)]
    bPS = [Buf(f"PS{b}") for b in range(8)]
    bRSTD = {}
    bP = Buf("P")
    bAUX = Buf("AUX")
    bONES = Buf("ONES")
    bGST = [[Buf(f"GST{i}_{f}") for f in range(NF)] for i in range(DEPTH)]
    bCVST = [[Buf(f"CVST{j}_{c}") for c in range(ND)] for j in range(2)]
    bHST = [Buf(f"HST{c}") for c in range(ND)]
    bUST = [Buf(f"UST{c}") for c in range(ND)]
    bOUT = Buf("outT")

    def gb(dct, pre, c, ti):
        k = (c, ti)
        if k not in dct:
            dct[k] = Buf(f"{pre}{c}_{ti}")
        return dct[k]

    dsW = [DSem(nc.alloc_semaphore(f"dw{s}")) for s in range(NSLOT)]
    dsX = [DSem(nc.alloc_semaphore(f"dx{c}")) for c in range(ND)]
    dsP = DSem(nc.alloc_semaphore("dpar"))
    dsAux = DSem(nc.alloc_semaphore("daux"))
    dsO = [DSem(nc.alloc_semaphore(f"do{s}")) for s in range(NNS)]

    state = {"ns": 0, "ps": 0, "wr": 0, "stg": 0}
    pinned = set()

    def ns():
        i = state["ns"]
        state["ns"] = (i + 1) % NNS
        return i

    def psb():
        while True:
            i = state["ps"]
            state["ps"] = (i + 1) % 8
            if i not in pinned:
                return i

    def stg():
        i = state["stg"]
        state["stg"] = (i + 1) % 3
        return i

    def wpiece(name):
        o, n = wlay[name]
        s = state["wr"]
        state["wr"] = (s + 1) % NSLOT
        pg.dma("pool", dsW[s], WR[s][:, 0:n], wall[:, o:o + n], reads=[], writes=[bWR[s]])
        return WR[s], bWR[s], n

    def pcol(name, c=0):
        o = play[name] + c
        return P[:, o:o + 1]

    pg.dma("sp", dsP, P[:, :], par[:, :], writes=[bP])
    pg.dma("sp", dsAux, AUX[:, :], auxd[:, :], writes=[bAUX])
    pg.op("dve", lambda e: e.memset(ONES[:, :], 1.0), writes=[bONES])
    pg.op("dve", lambda e: e.memset(GST[:, :, :], 0.0), writes=[b for l in bGST for b in l])
    pg.op("dve", lambda e: e.memset(CVST[:, :, :], 0.0), writes=[b for l in bCVST for b in l])
    pg.op("dve", lambda e: e.memset(HST[:, :, :], 0.0), writes=bHST)
    pg.op("dve", lambda e: e.memset(UST[:, :, :], 0.0), writes=bUST)

    pend = {"on": False}

    def stat_acc(c, first, off, w, ti):
        rb = gb(bRSTD, "RSTD", 0, ti)
        if first:
            pg.op("act", lambda e, c=c, off=off, w=w: e.activation(
                out=RSTD[:, off:off + w], in_=X[:, c, off:off + w], func=AF.Square),
                reads=[gb(bX, "X", c, ti)], writes=[rb])
        else:
            s = ns()
            pg.op("act", lambda e, s=s, c=c, off=off, w=w: e.activation(
                out=NS[s][:, 0:w], in_=X[:, c, off:off + w], func=AF.Square),
                reads=[gb(bX, "X", c, ti)], writes=[bNS[s]])
            pg.op("dve", lambda e, s=s, off=off, w=w: e.tensor_tensor(
                out=RSTD[:, off:off + w], in0=RSTD[:, off:off + w], in1=NS[s][:, 0:w], op=ALU.add),
                reads=[rb, bNS[s]], writes=[rb])

    def norm_stats(tiles, tis):
        if not pend["on"]:
            for (off, w), ti in zip(tiles, tis):
                for c in range(ND):
                    stat_acc(c, c == 0, off, w, ti)
        pend["on"] = False
        for (off, w), ti in zip(tiles, tis):
            rb = gb(bRSTD, "RSTD", 0, ti)
            b = psb()
            pg.mm_group(PS[b][:, 0:w], bPS[b], [(ONES[:, :], RSTD[:, off:off + w])], reads=[bONES, rb])
            s = ns()
            pg.op("act", lambda e, s=s, b=b, w=w: e.activation(
                out=NS[s][:, 0:w], in_=PS[b][:, 0:w], func=AF.Sqrt, bias=RMS_EPS, scale=1.0 / D),
                reads=[bPS[b]], writes=[bNS[s]])
            pg.op("dve", lambda e, s=s, off=off, w=w: e.reciprocal(out=RSTD[:, off:off + w], in_=NS[s][:, 0:w]),
                  reads=[bNS[s]], writes=[rb])

    def normalize_to_H(tiles, tis, gname):
        for (off, w), ti in zip(tiles, tis):
            for c in range(ND):
                pg.op("dve", lambda e, c=c, off=off, w=w: e.scalar_tensor_tensor(
                    out=H[:, c, off:off + w], in0=X[:, c, off:off + w], scalar=pcol(gname, c),
                    in1=RSTD[:, off:off + w], op0=ALU.mult, op1=ALU.mult),
                    reads=[gb(bX, "X", c, ti), bP, gb(bRSTD, "RSTD", 0, ti)], writes=[gb(bH, "H", c, ti)])

    def proj_group(wt, wb, nk, src, sbufs, off, w):
        b = psb()
        pairs = [(wt[:, k * 128:(k + 1) * 128], src(kk, off, w)) for k, kk in enumerate(nk)]
        pg.mm_group(PS[b][:, 0:w], bPS[b], pairs, reads=[wb] + sbufs)
        return b

    def conv3_chain(sg, off, w, wcols, bias_col):
        a = ns()
        base = MARG + off
        if bias_col is not None:
            pg.op("dve", lambda e: e.tensor_scalar(
                out=NS[a][:, 0:w], in0=STG[sg][:, base:base + w], scalar1=wcols[2], scalar2=bias_col,
                op0=ALU.mult, op1=ALU.add), reads=[bSTG[sg], bP], writes=[bNS[a]])
        else:
            pg.op("dve", lambda e: e.tensor_scalar(
                out=NS[a][:, 0:w], in0=STG[sg][:, base:base + w], scalar1=wcols[2], scalar2=None,
                op0=ALU.mult), reads=[bSTG[sg], bP], writes=[bNS[a]])
        for k in (1, 0):
            sh = 2 - k
            pg.op("dve", lambda e, k=k, sh=sh: e.scalar_tensor_tensor(
                out=NS[a][:, 0:w], in0=STG[sg][:, base - sh:base - sh + w], scalar=wcols[k],
                in1=NS[a][:, 0:w], op0=ALU.mult, op1=ALU.add),
                reads=[bSTG[sg], bP, bNS[a]], writes=[bNS[a]])
        return a

    def ffn(i, tiles, tis):
        S = tiles[-1][0] + tiles[-1][1]
        norm_stats(tiles, tis)
        normalize_to_H(tiles, tis, ("ffn_g", i))
        hb = lambda ti: [gb(bH, "H", c, ti) for c in range(ND)]
        for q in range(NQ):
            for fl in range(FQ):
                f = q * FQ + fl
                wg, wgb, _ = wpiece(("f_gate", i, f))
                wu, wub, _ = wpiece(("f_up", i, f))
                sg = stg()
                pg.op("act", lambda e, sg=sg, f=f: e.activation(
                    out=STG[sg][:, MARG - 2:MARG], in_=GST[:, i, 2 * f:2 * f + 2], func=AF.Copy),
                    reads=[bGST[i][f]], writes=[bSTG[sg]])
                wc = [pcol(("f_cw", i, k), f) for k in range(3)]
                cb = pcol(("f_cb", i), f)
                for (off, w), ti in zip(tiles, tis):
                    bg = proj_group(wg, wgb, range(ND), Hs, hb(ti), off, w)
                    bu = proj_group(wu, wub, range(ND), Hs, hb(ti), off, w)
                    pg.op("act", lambda e, sg=sg, bg=bg, off=off, w=w: e.activation(
                        out=STG[sg][:, MARG + off:MARG + off + w], in_=PS[bg][:, 0:w], func=AF.Copy),
                        reads=[bPS[bg]], writes=[bSTG[sg]])
                    a = conv3_chain(sg, off, w, wc, cb)
                    s = ns()
                    pg.op("act", lambda e, a=a, s=s, w=w: e.activation(
                        out=NS[s][:, 0:w], in_=NS[a][:, 0:w], func=AF.Silu),
                        reads=[bNS[a]], writes=[bNS[s]])
                    pg.op("dve", lambda e, s=s, bu=bu, fl=fl, off=off, w=w: e.tensor_tensor(
                        out=As(fl, off, w), in0=NS[s][:, 0:w], in1=PS[bu][:, 0:w], op=ALU.mult),
                        reads=[bNS[s], bPS[bu]], writes=[gb(bA, "A", fl, ti)])
                pg.op("act", lambda e, sg=sg, f=f: e.activation(
                    out=GST[:, i, 2 * f:2 * f + 2], in_=STG[sg][:, MARG + S - 2:MARG + S], func=AF.Copy),
                    reads=[bSTG[sg]], writes=[bGST[i][f]])
            for m in range(ND):
                wd, wdb, _ = wpiece(("f_down", i, q, m))
                for (off, w), ti in zip(tiles, tis):
                    by = proj_group(wd, wdb, range(FQ), As, [gb(bA, "A", fl, ti) for fl in range(FQ)], off, w)
                    pg.op("dve", lambda e, by=by, m=m, off=off, w=w: e.tensor_tensor(
                        out=X[:, m, off:off + w], in0=X[:, m, off:off + w], in1=PS[by][:, 0:w], op=ALU.add),
                        reads=[gb(bX, "X", m, ti), bPS[by]], writes=[gb(bX, "X", m, ti)])
                    if q == NQ - 1:
                        stat_acc(m, m == 0, off, w, ti)
        pend["on"] = True

    def mixer_a(i, tiles, tis):
        j = i // 3
        S = tiles[-1][0] + tiles[-1][1]
        norm_stats(tiles, tis)
        normalize_to_H(tiles, tis, ("mix_g", i))
        hb = lambda ti: [gb(bH, "H", c, ti) for c in range(ND)]
        for c in range(ND):
            wB, wBb, _ = wpiece(("a_in", j, c))
            wC, wCb, _ = wpiece(("a_in", j, ND + c))
            wV, wVb, _ = wpiece(("a_in", j, 2 * ND + c))
            sg = stg()
            pg.op("act", lambda e, sg=sg, c=c: e.activation(
                out=STG[sg][:, MARG - 2:MARG], in_=CVST[:, j, 2 * c:2 * c + 2], func=AF.Copy),
                reads=[bCVST[j][c]], writes=[bSTG[sg]])
            wc = [pcol(("a_cw", j, k), c) for k in range(3)]
            for (off, w), ti in zip(tiles, tis):
                bB = proj_group(wB, wBb, range(ND), Hs, hb(ti), off, w)
                bC = proj_group(wC, wCb, range(ND), Hs, hb(ti), off, w)
                bV = proj_group(wV, wVb, range(ND), Hs, hb(ti), off, w)
                s = ns()
                pg.op("act", lambda e, s=s, bC=bC, w=w: e.activation(
                    out=NS[s][:, 0:w], in_=PS[bC][:, 0:w], func=AF.Copy),
                    reads=[bPS[bC]], writes=[bNS[s]])
                pg.op("dve", lambda e, sg=sg, s=s, bV=bV, off=off, w=w: e.tensor_tensor(
                    out=STG[sg][:, MARG + off:MARG + off + w], in0=NS[s][:, 0:w], in1=PS[bV][:, 0:w],
                    op=ALU.mult), reads=[bNS[s], bPS[bV]], writes=[bSTG[sg]])
                a = conv3_chain(sg, off, w, wc, None)
                pg.op("dve", lambda e, a=a, bB=bB, c=c, off=off, w=w: e.tensor_tensor(
                    out=As(c, off, w), in0=NS[a][:, 0:w], in1=PS[bB][:, 0:w], op=ALU.mult),
                    reads=[bNS[a], bPS[bB]], writes=[gb(bA, "A", c, ti)])
            pg.op("act", lambda e, sg=sg, c=c: e.activation(
                out=CVST[:, j, 2 * c:2 * c + 2], in_=STG[sg][:, MARG + S - 2:MARG + S], func=AF.Copy),
                reads=[bSTG[sg]], writes=[bCVST[j][c]])
        for m in range(ND):
            wo, wob, _ = wpiece(("a_out", j, m))
            for (off, w), ti in zip(tiles, tis):
                by = proj_group(wo, wob, range(ND), As, [gb(bA, "A", c, ti) for c in range(ND)], off, w)
                pg.op("dve", lambda e, by=by, m=m, off=off, w=w: e.tensor_tensor(
                    out=X[:, m, off:off + w], in0=X[:, m, off:off + w], in1=PS[by][:, 0:w], op=ALU.add),
                    reads=[gb(bX, "X", m, ti), bPS[by]], writes=[gb(bX, "X", m, ti)])
                stat_acc(m, m == 0, off, w, ti)
        pend["on"] = True

    def mixer_b(i, tiles, tis, is_st0):
        S = tiles[-1][0] + tiles[-1][1]
        norm_stats(tiles, tis)
        for c in range(ND):
            g = c // 4
            win = POOL_WINDOWS[g]
            hbuf, pa, pb = stg(), stg(), stg()
            pg.op("act", lambda e, hbuf=hbuf, c=c: e.activation(
                out=STG[hbuf][:, MARG - 16:MARG], in_=HST[:, c, :], func=AF.Copy),
                reads=[bHST[c]], writes=[bSTG[hbuf]])
            for (off, w), ti in zip(tiles, tis):
                pg.op("dve", lambda e, hbuf=hbuf, c=c, off=off, w=w: e.scalar_tensor_tensor(
                    out=STG[hbuf][:, MARG + off:MARG + off + w], in0=X[:, c, off:off + w],
                    scalar=pcol(("mix_g", i), c), in1=RSTD[:, off:off + w], op0=ALU.mult, op1=ALU.mult),
                    reads=[gb(bX, "X", c, ti), bP, gb(bRSTD, "RSTD", 0, ti)], writes=[bSTG[hbuf]])
            lo = MARG - 16
            hi = MARG + S
            src, step, dst_cycle = hbuf, 1, [pa, pb]
            k = 0
            while step < win:
                dst = dst_cycle[k % 2]
                lo2 = lo + step
                pg.op("dve", lambda e, src=src, dst=dst, lo2=lo2, step=step: e.tensor_tensor(
                    out=STG[dst][:, lo2:hi], in0=STG[src][:, lo2:hi], in1=STG[src][:, lo2 - step:hi - step],
                    op=ALU.add), reads=[bSTG[src]], writes=[bSTG[dst]])
                src, lo, step, k = dst, lo2, step * 2, k + 1
            for (off, w), ti in zip(tiles, tis):
                pg.op("dve", lambda e, src=src, hbuf=hbuf, c=c, off=off, w=w, win=win: e.scalar_tensor_tensor(
                    out=As(c, off, w), in0=STG[src][:, MARG + off:MARG + off + w], scalar=1.0 / win,
                    in1=STG[hbuf][:, MARG + off:MARG + off + w], op0=ALU.mult, op1=ALU.subtract),
                    reads=[bSTG[src], bSTG[hbuf]], writes=[gb(bA, "A", c, ti)])
            if is_st0:
                t1 = [ti for (off_, w_), ti in zip(tiles, tis) if off_ <= HALO and HALO + 16 <= off_ + w_][0]
                s = ns()
                c0 = MARG + HALO
                pg.op("dve", lambda e, s=s, src=src, g=g: e.tensor_tensor(
                    out=NS[s][:, 0:16], in0=STG[src][:, c0:c0 + 16], in1=AUX[:, 64 + g * 16:64 + g * 16 + 16],
                    op=ALU.mult), reads=[bSTG[src], bAUX], writes=[bNS[s]])
                pg.op("dve", lambda e, s=s, hbuf=hbuf, c=c: e.tensor_tensor(
                    out=As(c, HALO, 16), in0=NS[s][:, 0:16], in1=STG[hbuf][:, c0:c0 + 16],
                    op=ALU.subtract), reads=[bNS[s], bSTG[hbuf]], writes=[gb(bA, "A", c, t1)])
            pg.op("act", lambda e, hbuf=hbuf, c=c: e.activation(
                out=HST[:, c, :], in_=STG[hbuf][:, MARG + S - 16:MARG + S], func=AF.Copy),
                reads=[bSTG[hbuf]], writes=[bHST[c]])
        for g in range(4):
            for m in range(4):
                wt, wb, _ = wpiece(("b_w", 0, g, m))
                cm = 4 * g + m
                for (off, w), ti in zip(tiles, tis):
                    by = proj_group(wt, wb, range(4 * g, 4 * g + 4), As,
                                    [gb(bA, "A", 4 * g + k, ti) for k in range(4)], off, w)
                    pg.op("dve", lambda e, by=by, cm=cm, off=off, w=w: e.scalar_tensor_tensor(
                        out=X[:, cm, off:off + w], in0=PS[by][:, 0:w], scalar=pcol(("b_scale",), cm),
                        in1=X[:, cm, off:off + w], op0=ALU.mult, op1=ALU.add),
                        reads=[bPS[by], bP, gb(bX, "X", cm, ti)], writes=[gb(bX, "X", cm, ti)])
                    stat_acc(cm, cm == 0, off, w, ti)
        pend["on"] = True

    def mixer_c(i, tiles, tis, is_st0):
        norm_stats(tiles, tis)
        normalize_to_H(tiles, tis, ("mix_g", i))
        bCO = Buf("COall")
        for tidx, ((off, w), ti) in enumerate(zip(tiles, tis)):
            halo_tile = is_st0 and tidx == 0
            hb = [gb(bH, "H", c, ti) for c in range(ND)]
            cobufs = [Buf(f"CO{c}") for c in range(ND)]
            allA = [b for b in bA.values()]
            for c in range(ND):
                wa, wab, _ = wpiece(("c_pw1", 0, c))
                wgt, wgb, _ = wpiece(("c_pw1", 0, ND + c))
                ba = proj_group(wa, wab, range(ND), Hs, hb, off, w)
                bg = proj_group(wgt, wgb, range(ND), Hs, hb, off, w)
                s = ns()
                pg.op("act", lambda e, s=s, bg=bg, c=c, w=w: e.activation(
                    out=NS[s][:, 0:w], in_=PS[bg][:, 0:w], func=AF.Sigmoid, bias=pcol(("c_b1",), ND + c)),
                    reads=[bPS[bg], bP], writes=[bNS[s]])
                ub = stg()
                pg.op("act", lambda e, ub=ub, c=c: e.activation(
                    out=STG[ub][:, MARG - (CW - 1):MARG], in_=UST[:, c, :], func=AF.Copy),
                    reads=[bUST[c]], writes=[bSTG[ub]])
                pg.op("dve", lambda e, ub=ub, s=s, ba=ba, c=c, w=w: e.scalar_tensor_tensor(
                    out=STG[ub][:, MARG:MARG + w], in0=PS[ba][:, 0:w], scalar=pcol(("c_b1",), c),
                    in1=NS[s][:, 0:w], op0=ALU.add, op1=ALU.mult),
                    reads=[bPS[ba], bP, bNS[s]], writes=[bSTG[ub]])
                if halo_tile:
                    pg.op("dve", lambda e, ub=ub: e.tensor_tensor(
                        out=STG[ub][:, MARG:MARG + HALO], in0=STG[ub][:, MARG:MARG + HALO], in1=AUX[:, 0:HALO],
                        op=ALU.mult), reads=[bSTG[ub], bAUX], writes=[bSTG[ub]])
                pg.op("act", lambda e, ub=ub, c=c, w=w: e.activation(
                    out=UST[:, c, :], in_=STG[ub][:, MARG + w - (CW - 1):MARG + w], func=AF.Copy),
                    reads=[bSTG[ub]], writes=[bUST[c]])
                co = CO[:, c * 512:c * 512 + w]
                a1 = ns()
                first = [True, True]
                for k in range(CW - 1, -1, -1):
                    ch = k % 2
                    dst = co if ch == 0 else NS[a1][:, 0:w]
                    dbuf = cobufs[c] if ch == 0 else bNS[a1]
                    src = STG[ub][:, MARG - (CW - 1) + k:MARG - (CW - 1) + k + w]
                    if first[ch]:
                        first[ch] = False
                        if ch == 0:
                            pg.op("dve", lambda e, dst=dst, src=src, k=k, c=c: e.tensor_scalar(
                                out=dst, in0=src, scalar1=pcol(("c_cw", k), c), scalar2=pcol(("c_cb",), c),
                                op0=ALU.mult, op1=ALU.add),
                                reads=[bSTG[ub], bP], writes=[dbuf] + (allA if c == 0 else []))
                        else:
                            pg.op("dve", lambda e, dst=dst, src=src, k=k, c=c: e.tensor_scalar(
                                out=dst, in0=src, scalar1=pcol(("c_cw", k), c), scalar2=None, op0=ALU.mult),
                                reads=[bSTG[ub], bP], writes=[dbuf])
                    else:
                        pg.op("dve", lambda e, dst=dst, src=src, k=k, c=c: e.scalar_tensor_tensor(
                            out=dst, in0=src, scalar=pcol(("c_cw", k), c), in1=dst, op0=ALU.mult, op1=ALU.add),
                            reads=[bSTG[ub], bP, dbuf], writes=[dbuf])
                pg.op("dve", lambda e, co=co, a1=a1, w=w: e.tensor_tensor(
                    out=co, in0=co, in1=NS[a1][:, 0:w], op=ALU.add),
                    reads=[cobufs[c], bNS[a1]], writes=[cobufs[c]])
            bsum, bsq = psb(), psb()
            for c in range(ND):
                co = CO[:, c * 512:c * 512 + w]
                s2 = ns()
                pg.op("act", lambda e, s2=s2, co=co, w=w: e.activation(
                    out=NS[s2][:, 0:w], in_=co, func=AF.Square), reads=[cobufs[c]], writes=[bNS[s2]])
                pg.mm_group(PS[bsum][:, 0:w], bPS[bsum], [(ONES[:, :], co)], reads=[bONES, cobufs[c]],
                            start=(c == 0), stop=(c == ND - 1))
                pg.mm_group(PS[bsq][:, 0:w], bPS[bsq], [(ONES[:, :], NS[s2][:, 0:w])], reads=[bONES, bNS[s2]],
                            start=(c == 0), stop=(c == ND - 1))
            mean, msq, var, rs = ns(), ns(), ns(), None
            pg.op("dve", lambda e, mean=mean, w=w, bsum=bsum: e.tensor_scalar(
                out=NS[mean][:, 0:w], in0=PS[bsum][:, 0:w], scalar1=1.0 / D, scalar2=None, op0=ALU.mult),
                reads=[bPS[bsum]], writes=[bNS[mean]])
            pg.op("dve", lambda e, mean=mean, msq=msq, w=w: e.tensor_tensor(
                out=NS[msq][:, 0:w], in0=NS[mean][:, 0:w], in1=NS[mean][:, 0:w], op=ALU.mult),
                reads=[bNS[mean]], writes=[bNS[msq]])
            pg.op("dve", lambda e, msq=msq, var=var, w=w, bsq=bsq: e.scalar_tensor_tensor(
                out=NS[var][:, 0:w], in0=PS[bsq][:, 0:w], scalar=1.0 / D, in1=NS[msq][:, 0:w],
                op0=ALU.mult, op1=ALU.subtract), reads=[bPS[bsq], bNS[msq]], writes=[bNS[var]])
            pg.op("act", lambda e, var=var, msq=msq, w=w: e.activation(
                out=NS[msq][:, 0:w], in_=NS[var][:, 0:w], func=AF.Sqrt, bias=LN_EPS, scale=1.0),
                reads=[bNS[var]], writes=[bNS[msq]])
            pg.op("dve", lambda e, var=var, msq=msq, w=w: e.reciprocal(out=NS[var][:, 0:w], in_=NS[msq][:, 0:w]),
                  reads=[bNS[msq]], writes=[bNS[var]])
            rstd = var
            keep = {mean, rstd}
            for c in range(ND):
                co = CO[:, c * 512:c * 512 + w]
                z = ns()
                while z in keep:
                    z = ns()
                pg.op("dve", lambda e, z=z, co=co, w=w, mean=mean: e.tensor_tensor(
                    out=NS[z][:, 0:w], in0=co, in1=NS[mean][:, 0:w], op=ALU.subtract),
                    reads=[cobufs[c], bNS[mean]], writes=[bNS[z]])
                pg.op("dve", lambda e, z=z, w=w, rstd=rstd: e.tensor_tensor(
                    out=NS[z][:, 0:w], in0=NS[z][:, 0:w], in1=NS[rstd][:, 0:w], op=ALU.mult),
                    reads=[bNS[z], bNS[rstd]], writes=[bNS[z]])
                pg.op("act", lambda e, z=z, c=c, off=off, w=w: e.activation(
                    out=H[:, c, off:off + w], in_=NS[z][:, 0:w], func=AF.Silu,
                    bias=pcol(("c_lnb",), c), scale=pcol(("c_lng",), c)),
                    reads=[bNS[z], bP], writes=[hb[c]])
            for m in range(ND):
                wt, wb, _ = wpiece(("c_pw2", 0, m))
                by = proj_group(wt, wb, range(ND), Hs, hb, off, w)
                pg.op("dve", lambda e, by=by, m=m, off=off, w=w: e.scalar_tensor_tensor(
                    out=X[:, m, off:off + w], in0=PS[by][:, 0:w], scalar=pcol(("c_b2",), m),
                    in1=X[:, m, off:off + w], op0=ALU.add, op1=ALU.add),
                    reads=[bPS[by], bP, gb(bX, "X", m, ti)], writes=[gb(bX, "X", m, ti)])
                if halo_tile:
                    pg.op("dve", lambda e, m=m, off=off: e.tensor_tensor(
                        out=X[:, m, off:off + HALO], in0=X[:, m, off:off + HALO], in1=AUX[:, 0:HALO], op=ALU.mult),
                        reads=[gb(bX, "X", m, ti), bAUX], writes=[gb(bX, "X", m, ti)])
                stat_acc(m, m == 0, off, w, ti)
            for b in bA.values():
                for cb_ in cobufs:
                    for sem, (val, src) in cb_.r.items():
                        cur = b.r.get(sem)
                        if cur is None or cur[0] < val:
                            b.r[sem] = (val, src)
                    if cb_.w is not None:
                        sem, val, src = cb_.w
                        cur = b.r.get(sem)
                        if cur is None or cur[0] < val:
                            b.r[sem] = (val, src)

        pend["on"] = True

    def final_norm(tiles, tis, st_tok0, is_st0):
        norm_stats(tiles, tis)
        for tidx, ((off, w), ti) in enumerate(zip(tiles, tis)):
            lo = max(off, HALO) if is_st0 else off
            if lo >= off + w:
                continue
            o0 = st_tok0 + lo - HALO
            wo = off + w - lo
            for c in range(ND):
                s = ns()
                pg.op("dve", lambda e, s=s, c=c, off=off, w=w: e.scalar_tensor_tensor(
                    out=NS[s][:, 0:w], in0=X[:, c, off:off + w], scalar=pcol(("fin_g",), c),
                    in1=RSTD[:, off:off + w], op0=ALU.mult, op1=ALU.mult),
                    reads=[gb(bX, "X", c, ti), bP, gb(bRSTD, "RSTD", 0, ti)], writes=[bNS[s]])
                pg.dma("sp", dsO[s], outT[c * 128:(c + 1) * 128, o0:o0 + wo], NS[s][:, lo - off:w],
                       reads=[bNS[s]], writes=[])

    def raw_out(tiles, tis, st_tok0, is_st0):
        for tidx, ((off, w), ti) in enumerate(zip(tiles, tis)):
            lo = max(off, HALO) if is_st0 else off
            if lo >= off + w:
                continue
            o0 = st_tok0 + lo - HALO
            for c in range(ND):
                pg.dma("sp", dsO[c % NNS], outT[c * 128:(c + 1) * 128, o0:o0 + off + w - lo], X[:, c, lo:off + w],
                       reads=[gb(bX, "X", c, ti)], writes=[])

    tok0 = 0
    tile_id = 0
    for si, tiles in enumerate(sts):
        tis = list(range(len(tiles)))
        is_st0 = si == 0
        S = st_w[si]
        pend["on"] = False
        for c in range(ND):
            pg.dma("sp", dsX[c], X[:, c, 0:S], xT[c * 128:(c + 1) * 128, tok0:tok0 + S],
                   writes=[gb(bX, "X", c, ti) for ti in tis])
        for i in layers:
            kind = i % 3
            if kind == 0:
                mixer_a(i, tiles, tis)
            elif kind == 1:
                mixer_b(i, tiles, tis, is_st0)
            else:
                mixer_c(i, tiles, tis, is_st0)
            if not skip_ffn:
                ffn(i, tiles, tis)
        if final:
            final_norm(tiles, tis, tok0, is_st0)
        else:
            raw_out(tiles, tis, tok0, is_st0)
        tok0 += S
    for d in dsO:
        if d.n:
            pg.eng["sp"].ops.append(("wait", d.sem, d.n))

    with nc.Block() as block:
        pg.emit(block)
    return nc, dict(NTOK=NTOK, NREAL=NREAL)


_CACHE = {}


def kernel(**inputs):
    inp = {k: np.asarray(v) for k, v in inputs.items()}
    x = inp["x"]
    B, S, _ = x.shape
    n_cores = 8
    halves = S // TOK_PER_CORE
    wallh = pack_wall(inp)
    parh = pack_params(inp)
    in_maps = []
    for core in range(n_cores):
        b, hf = core // halves, core % halves
        t0 = hf * TOK_PER_CORE
        xt = np.zeros((D, HALO + TOK_PER_CORE), np.float32)
        if hf > 0:
            xt[:, :HALO] = x[b, t0 - HALO:t0, :].T
        xt[:, HALO:] = x[b, t0:t0 + TOK_PER_CORE, :].T
        in_maps.append({"xT": xt, "wall": wallh, "par": parh, "aux": make_aux(hf == 0)})
    if "nc" not in _CACHE:
        _CACHE["nc"] = build_program()[0]
    res = run_bass_kernel_spmd(_CACHE["nc"], in_maps, core_ids=list(range(n_cores)))
    out = np.empty((B, S, D), np.float32)
    for core in range(n_cores):
        b, hf = core // halves, core % halves
        t0 = hf * TOK_PER_CORE
        out[b, t0:t0 + TOK_PER_CORE, :] = np.asarray(res.results[core]["outT"]).T
    return out
```

```python
import numpy as np
import concourse.bass as bass
import concourse.mybir as mybir
from concourse.bass_utils import run_bass_kernel_spmd

F32 = mybir.dt.float32
BF16 = mybir.dt.bfloat16
AF = mybir.ActivationFunctionType
ALU = mybir.AluOpType

D = 2048
F = 5632
ND = D // 128
NF = F // 128
NQ = 4
FQ = NF // NQ
DEPTH = 4
HALO = 64
TOK_PER_CORE = 4096
RMS_EPS = 1e-5
LN_EPS = 1e-5
POOL_WINDOWS = (2, 4, 8, 16)
CW = 31
MARG = 32
NSLOT = 7
SLOT_ELEMS = 2048
NNS = 5


def wall_layout():
    lay = {}
    off = 0

    def add(name, n):
        nonlocal off
        lay[name] = (off, n)
        off += n

    for i in range(DEPTH):
        kind, j = i % 3, i // 3
        if kind == 0:
            for nb in range(3 * ND):
                add(("a_in", j, nb), D)
            for m in range(ND):
                add(("a_out", j, m), D)
        elif kind == 1:
            for g in range(4):
                for m in range(4):
                    add(("b_w", j, g, m), 512)
        else:
            for nb in range(2 * ND):
                add(("c_pw1", j, nb), D)
            for m in range(ND):
                add(("c_pw2", j, m), D)
        for f in range(NF):
            add(("f_gate", i, f), D)
            add(("f_up", i, f), D)
        for q in range(NQ):
            for m in range(ND):
                add(("f_down", i, q, m), FQ * 128)
    return lay, off


def _blocks(W):
    K, N = W.shape
    return W.reshape(K // 128, 128, N // 128, 128).transpose(2, 1, 0, 3).reshape(N // 128, 128, K)


def pack_wall(inp):
    lay, total = wall_layout()
    wall = np.empty((128, total), np.float32)

    def put(name, arr):
        o, n = lay[name]
        assert arr.shape == (128, n), (name, arr.shape, n)
        wall[:, o:o + n] = arr

    for i in range(DEPTH):
        kind, j = i % 3, i // 3
        if kind == 0:
            b = _blocks(np.asarray(inp["a_w_in"][j]))
            for nb in range(3 * ND):
                put(("a_in", j, nb), b[nb])
            b = _blocks(np.asarray(inp["a_w_out"][j]))
            for m in range(ND):
                put(("a_out", j, m), b[m])
        elif kind == 1:
            for g in range(4):
                b = _blocks(np.asarray(inp["b_w_group"][j][g]))
                for m in range(4):
                    put(("b_w", j, g, m), b[m])
        else:
            b = _blocks(np.asarray(inp["c_w_pw1"][j]))
            for nb in range(2 * ND):
                put(("c_pw1", j, nb), b[nb])
            b = _blocks(np.asarray(inp["c_w_pw2"][j]))
            for m in range(ND):
                put(("c_pw2", j, m), b[m])
        b = _blocks(np.asarray(inp["f_w_gate"][i]))
        for f in range(NF):
            put(("f_gate", i, f), b[f])
        b = _blocks(np.asarray(inp["f_w_up"][i]))
        for f in range(NF):
            put(("f_up", i, f), b[f])
        wd = np.asarray(inp["f_w_down"][i])
        for q in range(NQ):
            b = _blocks(wd[q * FQ * 128:(q + 1) * FQ * 128, :])
            for m in range(ND):
                put(("f_down", i, q, m), b[m])
    return wall


def param_layout():
    lay = {}
    off = 0

    def add(name, ncols):
        nonlocal off
        lay[name] = off
        off += ncols

    for i in range(DEPTH):
        add(("mix_g", i), ND)
        add(("ffn_g", i), ND)
        for k in range(3):
            add(("f_cw", i, k), NF)
        add(("f_cb", i), NF)
    add(("fin_g",), ND)
    for j in range(2):
        for k in range(3):
            add(("a_cw", j, k), ND)
    add(("b_scale",), ND)
    add(("c_b1",), 2 * ND)
    for k in range(CW):
        add(("c_cw", k), ND)
    add(("c_cb",), ND)
    add(("c_lng",), ND)
    add(("c_lnb",), ND)
    add(("c_b2",), ND)
    return lay, off


def pack_params(inp):
    lay, total = param_layout()
    P = np.zeros((128, total), np.float32)

    def put(name, vec):
        vec = np.asarray(vec, np.float32).reshape(-1)
        n = vec.shape[0] // 128
        P[:, lay[name]:lay[name] + n] = vec.reshape(n, 128).T

    for i in range(DEPTH):
        put(("mix_g", i), inp["mix_norm_g"][i])
        put(("ffn_g", i), inp["ffn_norm_g"][i])
        for k in range(3):
            put(("f_cw", i, k), inp["f_conv_w"][i][k])
        put(("f_cb", i), inp["f_conv_b"][i])
    put(("fin_g",), inp["final_norm_g"])
    for j in range(2):
        for k in range(3):
            put(("a_cw", j, k), inp["a_conv_w"][j][k])
    put(("b_scale",), inp["b_scale"][0])
    put(("c_b1",), inp["c_b_pw1"][0])
    for k in range(CW):
        put(("c_cw", k), inp["c_conv_w"][0][k])
    put(("c_cb",), inp["c_conv_b"][0])
    put(("c_lng",), inp["c_ln_g"][0])
    put(("c_lnb",), inp["c_ln_b"][0])
    put(("c_b2",), inp["c_b_pw2"][0])
    return P


def make_aux(first_half):
    aux = np.zeros((128, 128), np.float32)
    aux[:, 0:64] = 0.0 if first_half else 1.0
    for g, w in enumerate(POOL_WINDOWS):
        for t in range(16):
            cnt = min(t + 1, w) if first_half else w
            aux[:, 64 + g * 16 + t] = np.float32(1.0) / np.float32(cnt)
    return aux


def default_sts():
    sts = [[(0, 384), (384, 352), (736, 352)]]
    for _ in range(3):
        sts.append([(0, 512), (512, 512)])
    return sts


class Buf:
    __slots__ = ("name", "w", "r")

    def __init__(self, name):
        self.name = name
        self.w = None
        self.r = {}


class Eng:
    def __init__(self, name, sem):
        self.name = name
        self.sem = sem
        self.n = 0
        self.ops = []
        self.waited = {}


class DSem:
    def __init__(self, sem):
        self.sem = sem
        self.n = 0


class Prog:
    def __init__(self, nc):
        self.nc = nc
        self.eng = {}
        for name in ("pe", "act", "dve", "pool", "sp"):
            self.eng[name] = Eng(name, nc.alloc_semaphore("sem_" + name))

    def _deps(self, eng, reads, writes, attach=False):
        need = {}

        def add(tok, raw):
            if tok is None:
                return
            sem, val, src = tok
            if src == eng.name and (not raw or eng.name == "pe"):
                return
            if need.get(sem, 0) < val:
                need[sem] = val

        for b in reads:
            add(b.w, True)
        for b in writes:
            add(b.w, False)
            for sem, (val, src) in b.r.items():
                add((sem, val, src), False)
        todo = []
        for sem, val in need.items():
            if eng.waited.get(sem, 0) < val:
                eng.waited[sem] = val
                todo.append((sem, val))
        att = None
        if attach and todo and eng.name in ("act", "dve"):
            att = todo.pop()
        for sem, val in todo:
            eng.ops.append(("wait", sem, val))
        return att

    def _commit(self, tok, reads, writes):
        sem, val, src = tok
        for b in reads:
            cur = b.r.get(sem)
            if cur is None or cur[0] < val:
                b.r[sem] = (val, src)
        for b in writes:
            b.w = tok
            b.r = {}

    def op(self, ename, fn, reads=(), writes=()):
        eng = self.eng[ename]
        att = self._deps(eng, reads, writes, attach=True)
        eng.n += 1
        eng.ops.append(("inc", fn, att))
        self._commit((eng.sem, eng.n, eng.name), reads, writes)

    def mm_group(self, out_ap, out_buf, pairs, reads, start=True, stop=True):
        eng = self.eng["pe"]
        self._deps(eng, reads, [out_buf])
        n = len(pairs)
        for k, (lhsT, rhs) in enumerate(pairs):
            st = start and k == 0
            sp = stop and k == n - 1
            fn = (lambda e, o=out_ap, l=lhsT, r=rhs, st=st, sp=sp: e.matmul(o, l, r, start=st, stop=sp))
            if k == n - 1:
                eng.n += 1
                eng.ops.append(("inc", fn, None))
            else:
                eng.ops.append(("plain", fn))
        self._commit((eng.sem, eng.n, eng.name), reads, [out_buf])

    def dma(self, qname, dsem, out_ap, in_ap, reads=(), writes=(), **kw):
        eng = self.eng[qname]
        self._deps(eng, reads, writes)
        dsem.n += 16
        eng.ops.append(("dma", out_ap, in_ap, dsem.sem, kw))
        self._commit((dsem.sem, dsem.n, "dma"), reads, writes)

    def final_wait(self, qname, bufs):
        eng = self.eng[qname]
        self._deps(eng, bufs, [])

    def emit(self, block):
        def run(ename):
            eng = self.eng[ename]

            def body(e):
                for o in eng.ops:
                    k = o[0]
                    if k == "wait":
                        e.wait_ge(o[1], o[2])
                    elif k == "inc":
                        ins = o[1](e)
                        if o[2] is not None:
                            ins._wait_ge(o[2][0], o[2][1])
                        ins.then_inc(eng.sem, 1)
                    elif k == "plain":
                        o[1](e)
                    else:
                        e.dma_start(out=o[1], in_=o[2], **o[4]).then_inc(o[3], 16)
            return body

        block.tensor(run("pe"))
        block.scalar(run("act"))
        block.vector(run("dve"))
        block.gpsimd(run("pool"))
        block.sync(run("sp"))


def build_program(sts=None, layers=(0, 1, 2, 3), final=True, skip_ffn=False):
    sts = sts or default_sts()
    st_w = [sum(w for _, w in st) for st in sts]
    SMAX = max(st_w)
    NTOK = sum(st_w)
    NREAL = NTOK - HALO
    wlay, wtotal = wall_layout()
    play, ptotal = param_layout()

    nc = bass.Bass("TRN2", target_bir_lowering=False)
    xT = nc.dram_tensor("xT", [D, NTOK], F32, kind="ExternalInput").ap()
    wall = nc.dram_tensor("wall", [128, wtotal], F32, kind="ExternalInput").ap()
    par = nc.dram_tensor("par", [128, ptotal], F32, kind="ExternalInput").ap()
    auxd = nc.dram_tensor("aux", [128, 128], F32, kind="ExternalInput").ap()
    outT = nc.dram_tensor("outT", [D, NREAL], F32, kind="ExternalOutput").ap()

    pg = Prog(nc)

    X = nc.alloc_sbuf_tensor("X", [128, ND, SMAX], F32)
    H = nc.alloc_sbuf_tensor("H", [128, ND, SMAX], BF16)
    CO = nc.alloc_sbuf_tensor("A", [128, ND * SMAX // 2], F32)
    Abf = CO.bitcast(BF16)

    def As(c, off, w):
        return Abf[:, c * SMAX + off:c * SMAX + off + w]

    def Hs(c, off, w):
        return H[:, c, off:off + w]
    WR = [nc.alloc_sbuf_tensor(f"WR{s}", [128, SLOT_ELEMS], BF16) for s in range(NSLOT)]
    STG = [nc.alloc_sbuf_tensor(f"STG{s}", [128, MARG + SMAX], F32) for s in range(3)]
    NS = [nc.alloc_sbuf_tensor(f"NS{s}", [128, 512], F32) for s in range(NNS)]
    RSTD = nc.alloc_sbuf_tensor("RSTD", [128, SMAX], F32)
    P = nc.alloc_sbuf_tensor("P", [128, ptotal], F32)
    AUX = nc.alloc_sbuf_tensor("AUX", [128, 128], F32)
    ONES = nc.alloc_sbuf_tensor("ONES", [128, 128], F32)
    GST = nc.alloc_sbuf_tensor("GST", [128, DEPTH, NF * 2], F32)
    CVST = nc.alloc_sbuf_tensor("CVST", [128, 2, ND * 2], F32)
    HST = nc.alloc_sbuf_tensor("HST", [128, ND, 16], F32)
    UST = nc.alloc_sbuf_tensor("UST", [128, ND, CW - 1], F32)
    PS = [nc.alloc_psum_tensor(f"PS{b}", [128, 512], F32) for b in range(8)]

    bX = {}
    bH = {}
    bA = {}
    bWR = [Buf(f"WR{s}") for s in range(NSLOT)]
    bSTG = [Buf(f"STG{s}") for s in range(3)]
    bNS = [Buf(f"NS{s}") for s in range(NNS)]
    bPS = [Buf(f"PS{b}") for b in range(8)]
    bRSTD = {}
    bP = Buf("P")
    bAUX = Buf("AUX")
    bONES = Buf("ONES")
    bGST = [[Buf(f"GST{i}_{f}") for f in range(NF)] for i in range(DEPTH)]
    bCVST = [[Buf(f"CVST{j}_{c}") for c in range(ND)] for j in range(2)]
    bHST = [Buf(f"HST{c}") for c in range(ND)]
    bUST = [Buf(f"UST{c}") for c in range(ND)]
    bOUT = Buf("outT")

    def gb(dct, pre, c, ti):
        k = (c, ti)
        if k not in dct:
            dct[k] = Buf(f"{pre}{c}_{ti}")
        return dct[k]

    dsW = [DSem(nc.alloc_semaphore(f"dw{s}")) for s in range(NSLOT)]
    dsX = [DSem(nc.alloc_semaphore(f"dx{c}")) for c in range(ND)]
    dsP = DSem(nc.alloc_semaphore("dpar"))
    dsAux = DSem(nc.alloc_semaphore("daux"))
    dsO = [DSem(nc.alloc_semaphore(f"do{s}")) for s in range(NNS)]

    state = {"ns": 0, "ps": 0, "wr": 0, "stg": 0}
    pinned = set()

    def ns():
        i = state["ns"]
        state["ns"] = (i + 1) % NNS
        return i

    def psb():
        while True:
            i = state["ps"]
            state["ps"] = (i + 1) % 8
            if i not in pinned:
                return i

    def stg():
        i = state["stg"]
        state["stg"] = (i + 1) % 3
        return i

    def wpiece(name):
        o, n = wlay[name]
        s = state["wr"]
        state["wr"] = (s + 1) % NSLOT
        pg.dma("pool", dsW[s], WR[s][:, 0:n], wall[:, o:o + n], reads=[], writes=[bWR[s]])
        return WR[s], bWR[s], n

    def pcol(name, c=0):
        o = play[name] + c
        return P[:, o:o + 1]

    pg.dma("sp", dsP, P[:, :], par[:, :], writes=[bP])
    pg.dma("sp", dsAux, AUX[:, :], auxd[:, :], writes=[bAUX])
    pg.op("dve", lambda e: e.memset(ONES[:, :], 1.0), writes=[bONES])
    pg.op("dve", lambda e: e.memset(GST[:, :, :], 0.0), writes=[b for l in bGST for b in l])
    pg.op("dve", lambda e: e.memset(CVST[:, :, :], 0.0), writes=[b for l in bCVST for b in l])
    pg.op("dve", lambda e: e.memset(HST[:, :, :], 0.0), writes=bHST)
    pg.op("dve", lambda e: e.memset(UST[:, :, :], 0.0), writes=bUST)

    pend = {"on": False}

    def stat_acc(c, first, off, w, ti):
        rb = gb(bRSTD, "RSTD", 0, ti)
        if first:
            pg.op("act", lambda e, c=c, off=off, w=w: e.activation(
                out=RSTD[:, off:off + w], in_=X[:, c, off:off + w], func=AF.Square),
                reads=[gb(bX, "X", c, ti)], writes=[rb])
        else:
            s = ns()
            pg.op("act", lambda e, s=s, c=c, off=off, w=w: e.activation(
                out=NS[s][:, 0:w], in_=X[:, c, off:off + w], func=AF.Square),
                reads=[gb(bX, "X", c, ti)], writes=[bNS[s]])
            pg.op("dve", lambda e, s=s, off=off, w=w: e.tensor_tensor(
                out=RSTD[:, off:off + w], in0=RSTD[:, off:off + w], in1=NS[s][:, 0:w], op=ALU.add),
                reads=[rb, bNS[s]], writes=[rb])

    def norm_stats(tiles, tis):
        if not pend["on"]:
            for (off, w), ti in zip(tiles, tis):
                for c in range(ND):
                    stat_acc(c, c == 0, off, w, ti)
        pend["on"] = False
        for (off, w), ti in zip(tiles, tis):
            rb = gb(bRSTD, "RSTD", 0, ti)
            b = psb()
            pg.mm_group(PS[b][:, 0:w], bPS[b], [(ONES[:, :], RSTD[:, off:off + w])], reads=[bONES, rb])
            s = ns()
            pg.op("act", lambda e, s=s, b=b, w=w: e.activation(
                out=NS[s][:, 0:w], in_=PS[b][:, 0:w], func=AF.Sqrt, bias=RMS_EPS, scale=1.0 / D),
                reads=[bPS[b]], writes=[bNS[s]])
            pg.op("dve", lambda e, s=s, off=off, w=w: e.reciprocal(out=RSTD[:, off:off + w], in_=NS[s][:, 0:w]),
                  reads=[bNS[s]], writes=[rb])

    def normalize_to_H(tiles, tis, gname):
        for (off, w), ti in zip(tiles, tis):
            for c in range(ND):
                pg.op("dve", lambda e, c=c, off=off, w=w: e.scalar_tensor_tensor(
                    out=H[:, c, off:off + w], in0=X[:, c, off:off + w], scalar=pcol(gname, c),
                    in1=RSTD[:, off:off + w], op0=ALU.mult, op1=ALU.mult),
                    reads=[gb(bX, "X", c, ti), bP, gb(bRSTD, "RSTD", 0, ti)], writes=[gb(bH, "H", c, ti)])

    def proj_group(wt, wb, nk, src, sbufs, off, w):
        b = psb()
        pairs = [(wt[:, k * 128:(k + 1) * 128], src(kk, off, w)) for k, kk in enumerate(nk)]
        pg.mm_group(PS[b][:, 0:w], bPS[b], pairs, reads=[wb] + sbufs)
        return b

    def conv3_chain(sg, off, w, wcols, bias_col):
        a = ns()
        base = MARG + off
        if bias_col is not None:
            pg.op("dve", lambda e: e.tensor_scalar(
                out=NS[a][:, 0:w], in0=STG[sg][:, base:base + w], scalar1=wcols[2], scalar2=bias_col,
                op0=ALU.mult, op1=ALU.add), reads=[bSTG[sg], bP], writes=[bNS[a]])
        else:
            pg.op("dve", lambda e: e.tensor_scalar(
                out=NS[a][:, 0:w], in0=STG[sg][:, base:base + w], scalar1=wcols[2], scalar2=None,
                op0=ALU.mult), reads=[bSTG[sg], bP], writes=[bNS[a]])
        for k in (1, 0):
            sh = 2 - k
            pg.op("dve", lambda e, k=k, sh=sh: e.scalar_tensor_tensor(
                out=NS[a][:, 0:w], in0=STG[sg][:, base - sh:base - sh + w], scalar=wcols[k],
                in1=NS[a][:, 0:w], op0=ALU.mult, op1=ALU.add),
                reads=[bSTG[sg], bP, bNS[a]], writes=[bNS[a]])
        return a

    def ffn(i, tiles, tis):
        S = tiles[-1][0] + tiles[-1][1]
        norm_stats(tiles, tis)
        normalize_to_H(tiles, tis, ("ffn_g", i))
        hb = lambda ti: [gb(bH, "H", c, ti) for c in range(ND)]
        for q in range(NQ):
            for fl in range(FQ):
                f = q * FQ + fl
                wg, wgb, _ = wpiece(("f_gate", i, f))
                wu, wub, _ = wpiece(("f_up", i, f))
                sg = stg()
                pg.op("act", lambda e, sg=sg, f=f: e.activation(
                    out=STG[sg][:, MARG - 2:MARG], in_=GST[:, i, 2 * f:2 * f + 2], func=AF.Copy),
                    reads=[bGST[i][f]], writes=[bSTG[sg]])
                wc = [pcol(("f_cw", i, k), f) for k in range(3)]
                cb = pcol(("f_cb", i), f)
                for (off, w), ti in zip(tiles, tis):
                    bg = proj_group(wg, wgb, range(ND), Hs, hb(ti), off, w)
                    bu = proj_group(wu, wub, range(ND), Hs, hb(ti), off, w)
                    pg.op("act", lambda e, sg=sg, bg=bg, off=off, w=w: e.activation(
                        out=STG[sg][:, MARG + off:MARG + off + w], in_=PS[bg][:, 0:w], func=AF.Copy),
                        reads=[bPS[bg]], writes=[bSTG[sg]])
                    a = conv3_chain(sg, off, w, wc, cb)
                    s = ns()
                    pg.op("act", lambda e, a=a, s=s, w=w: e.activation(
                        out=NS[s][:, 0:w], in_=NS[a][:, 0:w], func=AF.Silu),
                        reads=[bNS[a]], writes=[bNS[s]])
                    pg.op("dve", lambda e, s=s, bu=bu, fl=fl, off=off, w=w: e.tensor_tensor(
                        out=As(fl, off, w), in0=NS[s][:, 0:w], in1=PS[bu][:, 0:w], op=ALU.mult),
                        reads=[bNS[s], bPS[bu]], writes=[gb(bA, "A", fl, ti)])
                pg.op("act", lambda e, sg=sg, f=f: e.activation(
                    out=GST[:, i, 2 * f:2 * f + 2], in_=STG[sg][:, MARG + S - 2:MARG + S], func=AF.Copy),
                    reads=[bSTG[sg]], writes=[bGST[i][f]])
            for m in range(ND):
                wd, wdb, _ = wpiece(("f_down", i, q, m))
                for (off, w), ti in zip(tiles, tis):
                    by = proj_group(wd, wdb, range(FQ), As, [gb(bA, "A", fl, ti) for fl in range(FQ)], off, w)
                    pg.op("dve", lambda e, by=by, m=m, off=off, w=w: e.tensor_tensor(
                        out=X[:, m, off:off + w], in0=X[:, m, off:off + w], in1=PS[by][:, 0:w], op=ALU.add),
                        reads=[gb(bX, "X", m, ti), bPS[by]], writes=[gb(bX, "X", m, ti)])
                    if q == NQ - 1:
                        stat_acc(m, m == 0, off, w, ti)
        pend["on"] = True

    def mixer_a(i, tiles, tis):
        j = i // 3
        S = tiles[-1][0] + tiles[-1][1]
        norm_stats(tiles, tis)
        normalize_to_H(tiles, tis, ("mix_g", i))
        hb = lambda ti: [gb(bH, "H", c, ti) for c in range(ND)]
        for c in range(ND):
            wB, wBb, _ = wpiece(("a_in", j, c))
            wC, wCb, _ = wpiece(("a_in", j, ND + c))
            wV, wVb, _ = wpiece(("a_in", j, 2 * ND + c))
            sg = stg()
            pg.op("act", lambda e, sg=sg, c=c: e.activation(
                out=STG[sg][:, MARG - 2:MARG], in_=CVST[:, j, 2 * c:2 * c + 2], func=AF.Copy),
                reads=[bCVST[j][c]], writes=[bSTG[sg]])
            wc = [pcol(("a_cw", j, k), c) for k in range(3)]
            for (off, w), ti in zip(tiles, tis):
                bB = proj_group(wB, wBb, range(ND), Hs, hb(ti), off, w)
                bC = proj_group(wC, wCb, range(ND), Hs, hb(ti), off, w)
                bV = proj_group(wV, wVb, range(ND), Hs, hb(ti), off, w)
                s = ns()
                pg.op("act", lambda e, s=s, bC=bC, w=w: e.activation(
                    out=NS[s][:, 0:w], in_=PS[bC][:, 0:w], func=AF.Copy),
                    reads=[bPS[bC]], writes=[bNS[s]])
                pg.op("dve", lambda e, sg=sg, s=s, bV=bV, off=off, w=w: e.tensor_tensor(
                    out=STG[sg][:, MARG + off:MARG + off + w], in0=NS[s][:, 0:w], in1=PS[bV][:, 0:w],
                    op=ALU.mult), reads=[bNS[s], bPS[bV]], writes=[bSTG[sg]])
                a = conv3_chain(sg, off, w, wc, None)
                pg.op("dve", lambda e, a=a, bB=bB, c=c, off=off, w=w: e.tensor_tensor(
                    out=As(c, off, w), in0=NS[a][:, 0:w], in1=PS[bB][:, 0:w], op=ALU.mult),
                    reads=[bNS[a], bPS[bB]], writes=[gb(bA, "A", c, ti)])
            pg.op("act", lambda e, sg=sg, c=c: e.activation(
                out=CVST[:, j, 2 * c:2 * c + 2], in_=STG[sg][:, MARG + S - 2:MARG + S], func=AF.Copy),
                reads=[bSTG[sg]], writes=[bCVST[j][c]])
        for m in range(ND):
            wo, wob, _ = wpiece(("a_out", j, m))
            for (off, w), ti in zip(tiles, tis):
                by = proj_group(wo, wob, range(ND), As, [gb(bA, "A", c, ti) for c in range(ND)], off, w)
                pg.op("dve", lambda e, by=by, m=m, off=off, w=w: e.tensor_tensor(
                    out=X[:, m, off:off + w], in0=X[:, m, off:off + w], in1=PS[by][:, 0:w], op=ALU.add),
                    reads=[gb(bX, "X", m, ti), bPS[by]], writes=[gb(bX, "X", m, ti)])
                stat_acc(m, m == 0, off, w, ti)
        pend["on"] = True

    def mixer_b(i, tiles, tis, is_st0):
        S = tiles[-1][0] + tiles[-1][1]
        norm_stats(tiles, tis)
        deferred = []
        for g in range(4):
            for c in range(4 * g, 4 * g + 4):
                g = c // 4
                win = POOL_WINDOWS[g]
                hbuf, pa, pb = stg(), stg(), stg()
                pg.op("act", lambda e, hbuf=hbuf, c=c: e.activation(
                    out=STG[hbuf][:, MARG - 16:MARG], in_=HST[:, c, :], func=AF.Copy),
                    reads=[bHST[c]], writes=[bSTG[hbuf]])
                for (off, w), ti in zip(tiles, tis):
                    pg.op("dve", lambda e, hbuf=hbuf, c=c, off=off, w=w: e.scalar_tensor_tensor(
                        out=STG[hbuf][:, MARG + off:MARG + off + w], in0=X[:, c, off:off + w],
                        scalar=pcol(("mix_g", i), c), in1=RSTD[:, off:off + w], op0=ALU.mult, op1=ALU.mult),
                        reads=[gb(bX, "X", c, ti), bP, gb(bRSTD, "RSTD", 0, ti)], writes=[bSTG[hbuf]])
                lo = MARG - 16
                hi = MARG + S
                src, step, dst_cycle = hbuf, 1, [pa, pb]
                k = 0
                while step < win:
                    dst = dst_cycle[k % 2]
                    lo2 = lo + step
                    pg.op("dve", lambda e, src=src, dst=dst, lo2=lo2, step=step: e.tensor_tensor(
                        out=STG[dst][:, lo2:hi], in0=STG[src][:, lo2:hi], in1=STG[src][:, lo2 - step:hi - step],
                        op=ALU.add), reads=[bSTG[src]], writes=[bSTG[dst]])
                    src, lo, step, k = dst, lo2, step * 2, k + 1
                for (off, w), ti in zip(tiles, tis):
                    pg.op("dve", lambda e, src=src, hbuf=hbuf, c=c, off=off, w=w, win=win: e.scalar_tensor_tensor(
                        out=As(c, off, w), in0=STG[src][:, MARG + off:MARG + off + w], scalar=1.0 / win,
                        in1=STG[hbuf][:, MARG + off:MARG + off + w], op0=ALU.mult, op1=ALU.subtract),
                        reads=[bSTG[src], bSTG[hbuf]], writes=[gb(bA, "A", c, ti)])
                if is_st0:
                    t1 = [ti for (off_, w_), ti in zip(tiles, tis) if off_ <= HALO and HALO + 16 <= off_ + w_][0]
                    s = ns()
                    c0 = MARG + HALO
                    pg.op("dve", lambda e, s=s, src=src, g=g: e.tensor_tensor(
                        out=NS[s][:, 0:16], in0=STG[src][:, c0:c0 + 16], in1=AUX[:, 64 + g * 16:64 + g * 16 + 16],
                        op=ALU.mult), reads=[bSTG[src], bAUX], writes=[bNS[s]])
                    pg.op("dve", lambda e, s=s, hbuf=hbuf, c=c: e.tensor_tensor(
                        out=As(c, HALO, 16), in0=NS[s][:, 0:16], in1=STG[hbuf][:, c0:c0 + 16],
                        op=ALU.subtract), reads=[bNS[s], bSTG[hbuf]], writes=[gb(bA, "A", c, t1)])
                pg.op("act", lambda e, hbuf=hbuf, c=c: e.activation(
                    out=HST[:, c, :], in_=STG[hbuf][:, MARG + S - 16:MARG + S], func=AF.Copy),
                    reads=[bSTG[hbuf]], writes=[bHST[c]])
            if g == 3:
                for (cm_, off_, w_, ti_) in deferred:
                    stat_acc(cm_, cm_ == 0, off_, w_, ti_)
            for m in range(4):
                wt, wb, _ = wpiece(("b_w", 0, g, m))
                cm = 4 * g + m
                for (off, w), ti in zip(tiles, tis):
                    by = proj_group(wt, wb, range(4 * g, 4 * g + 4), As,
                                    [gb(bA, "A", 4 * g + k, ti) for k in range(4)], off, w)
                    pg.op("dve", lambda e, by=by, cm=cm, off=off, w=w: e.scalar_tensor_tensor(
                        out=X[:, cm, off:off + w], in0=PS[by][:, 0:w], scalar=pcol(("b_scale",), cm),
                        in1=X[:, cm, off:off + w], op0=ALU.mult, op1=ALU.add),
                        reads=[bPS[by], bP, gb(bX, "X", cm, ti)], writes=[gb(bX, "X", cm, ti)])
                    if g < 3:
                        deferred.append((cm, off, w, ti))
                    else:
                        stat_acc(cm, cm == 0, off, w, ti)
        pend["on"] = True

    def mixer_c(i, tiles, tis, is_st0):
        norm_stats(tiles, tis)
        normalize_to_H(tiles, tis, ("mix_g", i))
        bCO = Buf("COall")
        for tidx, ((off, w), ti) in enumerate(zip(tiles, tis)):
            halo_tile = is_st0 and tidx == 0
            hb = [gb(bH, "H", c, ti) for c in range(ND)]
            cobufs = [Buf(f"CO{c}") for c in range(ND)]
            bsum, bsq = psb(), psb()
            pinned.add(bsum)
            pinned.add(bsq)

            def ln_stat(c, bsum=bsum, bsq=bsq, w=w, cobufs=cobufs):
                co = CO[:, c * 512:c * 512 + w]
                s2 = ns()
                pg.op("act", lambda e, s2=s2, co=co, w=w: e.activation(
                    out=NS[s2][:, 0:w], in_=co, func=AF.Square), reads=[cobufs[c]], writes=[bNS[s2]])
                pg.mm_group(PS[bsum][:, 0:w], bPS[bsum], [(ONES[:, :], co)], reads=[bONES, cobufs[c]],
                            start=(c == 0), stop=(c == ND - 1))
                pg.mm_group(PS[bsq][:, 0:w], bPS[bsq], [(ONES[:, :], NS[s2][:, 0:w])], reads=[bONES, bNS[s2]],
                            start=(c == 0), stop=(c == ND - 1))
            allA = [b for b in bA.values()]
            for c in range(ND):
                wa, wab, _ = wpiece(("c_pw1", 0, c))
                wgt, wgb, _ = wpiece(("c_pw1", 0, ND + c))
                ba = proj_group(wa, wab, range(ND), Hs, hb, off, w)
                bg = proj_group(wgt, wgb, range(ND), Hs, hb, off, w)
                if c >= 2:
                    ln_stat(c - 2)
                s = ns()
                pg.op("act", lambda e, s=s, bg=bg, c=c, w=w: e.activation(
                    out=NS[s][:, 0:w], in_=PS[bg][:, 0:w], func=AF.Sigmoid, bias=pcol(("c_b1",), ND + c)),
                    reads=[bPS[bg], bP], writes=[bNS[s]])
                ub = stg()
                pg.op("act", lambda e, ub=ub, c=c: e.activation(
                    out=STG[ub][:, MARG - (CW - 1):MARG], in_=UST[:, c, :], func=AF.Copy),
                    reads=[bUST[c]], writes=[bSTG[ub]])
                pg.op("dve", lambda e, ub=ub, s=s, ba=ba, c=c, w=w: e.scalar_tensor_tensor(
                    out=STG[ub][:, MARG:MARG + w], in0=PS[ba][:, 0:w], scalar=pcol(("c_b1",), c),
                    in1=NS[s][:, 0:w], op0=ALU.add, op1=ALU.mult),
                    reads=[bPS[ba], bP, bNS[s]], writes=[bSTG[ub]])
                if halo_tile:
                    pg.op("dve", lambda e, ub=ub: e.tensor_tensor(
                        out=STG[ub][:, MARG:MARG + HALO], in0=STG[ub][:, MARG:MARG + HALO], in1=AUX[:, 0:HALO],
                        op=ALU.mult), reads=[bSTG[ub], bAUX], writes=[bSTG[ub]])
                pg.op("act", lambda e, ub=ub, c=c, w=w: e.activation(
                    out=UST[:, c, :], in_=STG[ub][:, MARG + w - (CW - 1):MARG + w], func=AF.Copy),
                    reads=[bSTG[ub]], writes=[bUST[c]])
                co = CO[:, c * 512:c * 512 + w]
                a1 = ns()
                first = [True, True]
                for k in range(CW - 1, -1, -1):
                    ch = k % 2
                    dst = co if ch == 0 else NS[a1][:, 0:w]
                    dbuf = cobufs[c] if ch == 0 else bNS[a1]
                    src = STG[ub][:, MARG - (CW - 1) + k:MARG - (CW - 1) + k + w]
                    if first[ch]:
                        first[ch] = False
                        if ch == 0:
                            pg.op("dve", lambda e, dst=dst, src=src, k=k, c=c: e.tensor_scalar(
                                out=dst, in0=src, scalar1=pcol(("c_cw", k), c), scalar2=pcol(("c_cb",), c),
                                op0=ALU.mult, op1=ALU.add),
                                reads=[bSTG[ub], bP], writes=[dbuf] + (allA if c == 0 else []))
                        else:
                            pg.op("dve", lambda e, dst=dst, src=src, k=k, c=c: e.tensor_scalar(
                                out=dst, in0=src, scalar1=pcol(("c_cw", k), c), scalar2=None, op0=ALU.mult),
                                reads=[bSTG[ub], bP], writes=[dbuf])
                    else:
                        pg.op("dve", lambda e, dst=dst, src=src, k=k, c=c: e.scalar_tensor_tensor(
                            out=dst, in0=src, scalar=pcol(("c_cw", k), c), in1=dst, op0=ALU.mult, op1=ALU.add),
                            reads=[bSTG[ub], bP, dbuf], writes=[dbuf])
                pg.op("dve", lambda e, co=co, a1=a1, w=w: e.tensor_tensor(
                    out=co, in0=co, in1=NS[a1][:, 0:w], op=ALU.add),
                    reads=[cobufs[c], bNS[a1]], writes=[cobufs[c]])
            ln_stat(ND - 2)
            ln_stat(ND - 1)
            mean, msq, var, rs = ns(), ns(), ns(), None
            pg.op("dve", lambda e, mean=mean, w=w, bsum=bsum: e.tensor_scalar(
                out=NS[mean][:, 0:w], in0=PS[bsum][:, 0:w], scalar1=1.0 / D, scalar2=None, op0=ALU.mult),
                reads=[bPS[bsum]], writes=[bNS[mean]])
            pg.op("dve", lambda e, mean=mean, msq=msq, w=w: e.tensor_tensor(
                out=NS[msq][:, 0:w], in0=NS[mean][:, 0:w], in1=NS[mean][:, 0:w], op=ALU.mult),
                reads=[bNS[mean]], writes=[bNS[msq]])
            pg.op("dve", lambda e, msq=msq, var=var, w=w, bsq=bsq: e.scalar_tensor_tensor(
                out=NS[var][:, 0:w], in0=PS[bsq][:, 0:w], scalar=1.0 / D, in1=NS[msq][:, 0:w],
                op0=ALU.mult, op1=ALU.subtract), reads=[bPS[bsq], bNS[msq]], writes=[bNS[var]])
            pg.op("act", lambda e, var=var, msq=msq, w=w: e.activation(
                out=NS[msq][:, 0:w], in_=NS[var][:, 0:w], func=AF.Sqrt, bias=LN_EPS, scale=1.0),
                reads=[bNS[var]], writes=[bNS[msq]])
            pg.op("dve", lambda e, var=var, msq=msq, w=w: e.reciprocal(out=NS[var][:, 0:w], in_=NS[msq][:, 0:w]),
                  reads=[bNS[msq]], writes=[bNS[var]])
            pinned.discard(bsum)
            pinned.discard(bsq)
            rstd = var
            keep = {mean, rstd}
            for c in range(ND):
                co = CO[:, c * 512:c * 512 + w]
                z = ns()
                while z in keep:
                    z = ns()
                pg.op("dve", lambda e, z=z, co=co, w=w, mean=mean: e.tensor_tensor(
                    out=NS[z][:, 0:w], in0=co, in1=NS[mean][:, 0:w], op=ALU.subtract),
                    reads=[cobufs[c], bNS[mean]], writes=[bNS[z]])
                pg.op("dve", lambda e, z=z, w=w, rstd=rstd: e.tensor_tensor(
                    out=NS[z][:, 0:w], in0=NS[z][:, 0:w], in1=NS[rstd][:, 0:w], op=ALU.mult),
                    reads=[bNS[z], bNS[rstd]], writes=[bNS[z]])
                pg.op("act", lambda e, z=z, c=c, off=off, w=w: e.activation(
                    out=H[:, c, off:off + w], in_=NS[z][:, 0:w], func=AF.Silu,
                    bias=pcol(("c_lnb",), c), scale=pcol(("c_lng",), c)),
                    reads=[bNS[z], bP], writes=[hb[c]])
            for m in range(ND):
                wt, wb, _ = wpiece(("c_pw2", 0, m))
                by = proj_group(wt, wb, range(ND), Hs, hb, off, w)
                pg.op("dve", lambda e, by=by, m=m, off=off, w=w: e.scalar_tensor_tensor(
                    out=X[:, m, off:off + w], in0=PS[by][:, 0:w], scalar=pcol(("c_b2",), m),
                    in1=X[:, m, off:off + w], op0=ALU.add, op1=ALU.add),
                    reads=[bPS[by], bP, gb(bX, "X", m, ti)], writes=[gb(bX, "X", m, ti)])
                if halo_tile:
                    pg.op("dve", lambda e, m=m, off=off: e.tensor_tensor(
                        out=X[:, m, off:off + HALO], in0=X[:, m, off:off + HALO], in1=AUX[:, 0:HALO], op=ALU.mult),
                        reads=[gb(bX, "X", m, ti), bAUX], writes=[gb(bX, "X", m, ti)])
                stat_acc(m, m == 0, off, w, ti)
            for b in bA.values():
                for cb_ in cobufs:
                    for sem, (val, src) in cb_.r.items():
                        cur = b.r.get(sem)
                        if cur is None or cur[0] < val:
                            b.r[sem] = (val, src)
                    if cb_.w is not None:
                        sem, val, src = cb_.w
                        cur = b.r.get(sem)
                        if cur is None or cur[0] < val:
                            b.r[sem] = (val, src)

        pend["on"] = True

    def final_norm(tiles, tis, st_tok0, is_st0):
        norm_stats(tiles, tis)
        for tidx, ((off, w), ti) in enumerate(zip(tiles, tis)):
            lo = max(off, HALO) if is_st0 else off
            if lo >= off + w:
                continue
            o0 = st_tok0 + lo - HALO
            wo = off + w - lo
            for c in range(ND):
                s = ns()
                pg.op("dve", lambda e, s=s, c=c, off=off, w=w: e.scalar_tensor_tensor(
                    out=NS[s][:, 0:w], in0=X[:, c, off:off + w], scalar=pcol(("fin_g",), c),
                    in1=RSTD[:, off:off + w], op0=ALU.mult, op1=ALU.mult),
                    reads=[gb(bX, "X", c, ti), bP, gb(bRSTD, "RSTD", 0, ti)], writes=[bNS[s]])
                pg.dma("sp", dsO[s], outT[c * 128:(c + 1) * 128, o0:o0 + wo], NS[s][:, lo - off:w],
                       reads=[bNS[s]], writes=[])

    def raw_out(tiles, tis, st_tok0, is_st0):
        for tidx, ((off, w), ti) in enumerate(zip(tiles, tis)):
            lo = max(off, HALO) if is_st0 else off
            if lo >= off + w:
                continue
            o0 = st_tok0 + lo - HALO
            for c in range(ND):
                pg.dma("sp", dsO[c % NNS], outT[c * 128:(c + 1) * 128, o0:o0 + off + w - lo], X[:, c, lo:off + w],
                       reads=[gb(bX, "X", c, ti)], writes=[])

    tok0 = 0
    tile_id = 0
    for si, tiles in enumerate(sts):
        tis = list(range(len(tiles)))
        is_st0 = si == 0
        S = st_w[si]
        pend["on"] = False
        for c in range(ND):
            pg.dma("sp", dsX[c], X[:, c, 0:S], xT[c * 128:(c + 1) * 128, tok0:tok0 + S],
                   writes=[gb(bX, "X", c, ti) for ti in tis])
        for i in layers:
            kind = i % 3
            if kind == 0:
                mixer_a(i, tiles, tis)
            elif kind == 1:
                mixer_b(i, tiles, tis, is_st0)
            else:
                mixer_c(i, tiles, tis, is_st0)
            if not skip_ffn:
                ffn(i, tiles, tis)
        if final:
            final_norm(tiles, tis, tok0, is_st0)
        else:
            raw_out(tiles, tis, tok0, is_st0)
        tok0 += S
    for d in dsO:
        if d.n:
            pg.eng["sp"].ops.append(("wait", d.sem, d.n))

    with nc.Block() as block:
        pg.emit(block)
    return nc, dict(NTOK=NTOK, NREAL=NREAL)


_CACHE = {}


def kernel(**inputs):
    inp = {k: np.asarray(v) for k, v in inputs.items()}
    x = inp["x"]
    B, S, _ = x.shape
    n_cores = 8
    halves = S // TOK_PER_CORE
    wallh = pack_wall(inp)
    parh = pack_params(inp)
    in_maps = []
    for core in range(n_cores):
        b, hf = core // halves, core % halves
        t0 = hf * TOK_PER_CORE
        xt = np.zeros((D, HALO + TOK_PER_CORE), np.float32)
        if hf > 0:
            xt[:, :HALO] = x[b, t0 - HALO:t0, :].T
        xt[:, HALO:] = x[b, t0:t0 + TOK_PER_CORE, :].T
        in_maps.append({"xT": xt, "wall": wallh, "par": parh, "aux": make_aux(hf == 0)})
    if "nc" not in _CACHE:
        _CACHE["nc"] = build_program()[0]
    res = run_bass_kernel_spmd(_CACHE["nc"], in_maps, core_ids=list(range(n_cores)))
    out = np.empty((B, S, D), np.float32)
    for core in range(n_cores):
        b, hf = core // halves, core % halves
        t0 = hf * TOK_PER_CORE
        out[b, t0:t0 + TOK_PER_CORE, :] = np.asarray(res.results[core]["outT"]).T
    return out
```
